# Optimizing a Trainium2 kernel written in Bass

```python
import jax, jax.numpy as jnp
from jax import lax
import numpy as np

D_MODEL = 2048
BATCH = 2
SEQ = 8192
DEPTH = 4
DEC_BATCH = 16
DEC_SEQ = 2048
PAST_LEN = 128

N_TOKEN_MIXERS = 2
ATTN_WINDOWS = ((128, 1), (512, 4), (2048, 16))
N_ATTN_GROUPS = 3
HEADS_PER_GROUP = 6
HEAD_DIM = 128
N_ATTN_HEADS = N_ATTN_GROUPS * HEADS_PER_GROUP
ATTN_WIDTH = N_ATTN_HEADS * HEAD_DIM
QKV_WIDTH = 3 * ATTN_WIDTH
N_FOURIER_GROUPS = 8
FOURIER_GROUP_DIM = D_MODEL // N_FOURIER_GROUPS
D_FF = 4 * D_MODEL
N_ATTN_LAYERS = (DEPTH + 1) // 2
N_FOURIER_LAYERS = DEPTH // 2
N_MOD = 6
RMS_EPS = 1e-6
MASK_VALUE = -1e30

kernel_name = "dilated_attn_fnet_adaln_hybrid_encoder"


def rmsnorm(x, gain):
    xf = x.astype(jnp.float32)
    xf = xf * lax.rsqrt(jnp.mean(xf * xf, axis=-1, keepdims=True) + RMS_EPS)
    return xf.astype(x.dtype) * gain


def ada_norm(x, gain, shift, scale):
    return rmsnorm(x, gain) * (1 + scale[:, None, :]) + shift[:, None, :]


def alibi_slopes():
    h = jnp.arange(1, N_ATTN_HEADS + 1, dtype=jnp.float32)
    return jnp.exp2(-8.0 * h / N_ATTN_HEADS)


def dilated_window_attention(q, k, v, dilation, half, slopes):
    B, S, H, E = q.shape
    blk = dilation * half
    s_pad = -(-S // blk) * blk
    pad = s_pad - S
    n_u = s_pad // dilation
    nb = n_u // half

    def to_sub(t):
        t = jnp.pad(t, ((0, 0), (0, pad), (0, 0), (0, 0)))
        t = t.reshape(B, n_u, dilation, H, E).transpose(0, 2, 1, 3, 4)
        return t.reshape(B, dilation, nb, half, H, E)

    def neighbours(t):
        z = jnp.zeros_like(t[:, :, :1])
        prev = jnp.concatenate([z, t[:, :, :-1]], axis=2)
        nxt = jnp.concatenate([t[:, :, 1:], z], axis=2)
        return jnp.concatenate([prev, t, nxt], axis=3)

    qb = to_sub(q)
    kn = neighbours(to_sub(k))
    vn = neighbours(to_sub(v))

    scores = jnp.einsum('brnqhe,brnkhe->brnhqk', qb, kn,
                        preferred_element_type=jnp.float32) * (E ** -0.5)
    qi = jnp.arange(half)
    kj = jnp.arange(3 * half) - half
    du = kj[None, :] - qi[:, None]
    u_key = jnp.arange(nb)[:, None] * half + kj[None, :]
    pos_key = u_key[None] * dilation + jnp.arange(dilation)[:, None, None]
    key_ok = (u_key[None] >= 0) & (pos_key < S)
    valid = key_ok[:, :, None, :] & (jnp.abs(du) <= half)[None, None]
    dist = (jnp.abs(du) * dilation).astype(jnp.float32)
    scores = scores - slopes[:, None, None] * dist[None]
    scores = jnp.where(valid[None, :, :, None], scores, MASK_VALUE)
    lse = jax.nn.logsumexp(scores, axis=-1)
    p = jnp.exp(scores - lse[..., None])
    out = jnp.einsum('brnhqk,brnkhe->brnqhe', p.astype(vn.dtype), vn)
    out = out.reshape(B, dilation, n_u, H, E).transpose(0, 2, 1, 3, 4).reshape(B, s_pad, H, E)[:, :S]
    lse = lse.transpose(0, 1, 2, 4, 3).reshape(B, dilation, n_u, H)
    lse = lse.transpose(0, 2, 1, 3).reshape(B, s_pad, H)[:, :S]
    return out, lse


def dilated_attention_mixer(h, w_qkv, w_o):
    B, S, _ = h.shape
    qkv = (h @ w_qkv).reshape(B, S, N_ATTN_GROUPS, 3, HEADS_PER_GROUP, HEAD_DIM)
    slopes = alibi_slopes()
    outs, lses = [], []
    for g, (window, dilation) in enumerate(ATTN_WINDOWS):
        half = window // (2 * dilation)
        o, l = dilated_window_attention(qkv[:, :, g, 0], qkv[:, :, g, 1], qkv[:, :, g, 2], dilation, half,
                                        slopes[g * HEADS_PER_GROUP:(g + 1) * HEADS_PER_GROUP])
        outs.append(o)
        lses.append(l)
    alpha = jax.nn.softmax(jnp.stack(lses, axis=0), axis=0)
    mixed = jnp.concatenate([(alpha[g][..., None] * outs[g]).astype(h.dtype)
                             for g in range(N_ATTN_GROUPS)], axis=2)
    return mixed.reshape(B, S, ATTN_WIDTH) @ w_o


def fourier_mixer(h, w_f, b_f):
    B, S, D = h.shape
    hg = h.astype(jnp.float32).reshape(B, S, N_FOURIER_GROUPS, FOURIER_GROUP_DIM)
    mixed = jnp.fft.fft2(hg, axes=(1, 3), norm="ortho").real
    return mixed.reshape(B, S, D).astype(h.dtype) @ w_f + b_f


def sqrelu_mlp(h, w1, b1, w2, b2):
    u = jax.nn.relu(h @ w1 + b1)
    return jnp.square(u) @ w2 + b2


def trunk(x, c, w_ada, b_ada, norm1_g, norm2_g, w_qkv, w_o, w_f, b_f, w1, b1, w2, b2, final_g):
    c_act = jax.nn.silu(c)
    for i in range(DEPTH):
        mod = c_act @ w_ada[i] + b_ada[i]
        sh1, sc1, g1, sh2, sc2, g2 = jnp.split(mod, N_MOD, axis=-1)
        h = ada_norm(x, norm1_g[i], sh1, sc1)
        if i % N_TOKEN_MIXERS == 0:
            y = dilated_attention_mixer(h, w_qkv[i // N_TOKEN_MIXERS], w_o[i // N_TOKEN_MIXERS])
        else:
            y = fourier_mixer(h, w_f[i // N_TOKEN_MIXERS], b_f[i // N_TOKEN_MIXERS])
        x = x + g1[:, None, :] * y
        h = ada_norm(x, norm2_g[i], sh2, sc2)
        x = x + g2[:, None, :] * sqrelu_mlp(h, w1[i], b1[i], w2[i], b2[i])
    return rmsnorm(x, final_g)


def setup_inputs(seed: int = 0) -> dict:
    key = jax.random.key(seed)
    ks = jax.random.split(key, 20)
    f32 = jnp.float32
    n = lambda k, shape, s: jax.random.normal(k, shape, f32) * s
    D = D_MODEL
    return {
        "x_prompt": n(ks[0], (BATCH, SEQ, D), 1.0),
        "x_sample": n(ks[1], (DEC_BATCH, DEC_SEQ, D), 1.0),
        "c_prompt": n(ks[2], (BATCH, D), 1.0),
        "c_sample": n(ks[3], (DEC_BATCH, D), 1.0),
        "w_ada": n(ks[4], (DEPTH, D, N_MOD * D), D ** -0.5),
        "b_ada": n(ks[5], (DEPTH, N_MOD * D), 0.02),
        "norm1_g": 1.0 + n(ks[6], (DEPTH, D), 0.02),
        "norm2_g": 1.0 + n(ks[7], (DEPTH, D), 0.02),
        "w_qkv": n(ks[8], (N_ATTN_LAYERS, D, QKV_WIDTH), D ** -0.5),
        "w_o": n(ks[9], (N_ATTN_LAYERS, ATTN_WIDTH, D), ATTN_WIDTH ** -0.5),
        "w_f": n(ks[10], (N_FOURIER_LAYERS, D, D), D ** -0.5),
        "b_f": n(ks[11], (N_FOURIER_LAYERS, D), 0.02),
        "w1": n(ks[12], (DEPTH, D, D_FF), D ** -0.5),
        "b1": n(ks[13], (DEPTH, D_FF), 0.02),
        "w2": n(ks[14], (DEPTH, D_FF, D), D_FF ** -0.5),
        "b2": n(ks[15], (DEPTH, D), 0.02),
        "final_g": 1.0 + n(ks[16], (D,), 0.02),
    }


def reference(x_prompt, x_sample, c_prompt, c_sample, w_ada, b_ada, norm1_g, norm2_g, w_qkv, w_o,
              w_f, b_f, w1, b1, w2, b2, final_g):
    y_prompt = trunk(x_prompt, c_prompt, w_ada, b_ada, norm1_g, norm2_g, w_qkv, w_o, w_f, b_f,
                     w1, b1, w2, b2, final_g)
    y_sample = trunk(x_sample, c_sample, w_ada, b_ada, norm1_g, norm2_g, w_qkv, w_o, w_f, b_f,
                     w1, b1, w2, b2, final_g)
    return (y_prompt, y_sample)
```

```python
import numpy as np
from contextlib import ExitStack
import ml_dtypes
import concourse.bass as bass
import concourse.mybir as mybir
from concourse.bass_utils import run_bass_kernel_spmd

F32 = mybir.dt.float32
BF16 = mybir.dt.bfloat16
AF = mybir.ActivationFunctionType
ALU = mybir.AluOpType

D = 2048
KC = 16
NSLOT = 4
SLOT = 2048
NTOK = 8192
DFF = 8192
QKVW = 6912
AW = 2304
NL = 4
PADT = 1024
EPS = 1e-6
NEG = -30000.0
DIL = (1, 4, 16)
N_CORES = 8


def sl(start, n, step=1):
    return slice(start, start + (n - 1) * step + 1, step)


def _flat(evs):
    out = []
    for e in evs:
        if e is None:
            continue
        if isinstance(e, tuple) and len(e) == 2 and isinstance(e[0], str):
            out.append(e)
        else:
            out.extend(_flat(e))
    return out


class Gen:
    def __init__(self, nc):
        self.nc = nc
        self.eng = {"pe": nc.tensor, "act": nc.scalar, "dve": nc.vector, "pool": nc.gpsimd, "sp": nc.sync}
        self.sems = {}
        self.cnt = {}
        self.waited = {e: {} for e in self.eng}
        self.last = {}
        self.bank_free = [None] * 8
        self.bank_i = 0
        self.banks = None

    def sem(self, key):
        if key not in self.sems:
            self.sems[key] = self.nc.alloc_semaphore(key)
            self.cnt[key] = 0
        return self.sems[key]

    def wait(self, eng, evs):
        for key, val in _flat(evs):
            if self.waited[eng].get(key, 0) >= val:
                continue
            self.eng[eng].wait_ge(self.sems[key], val)
            self.waited[eng][key] = val

    def op(self, eng, fn, waits=(), sig=True):
        self.wait(eng, waits)
        ins = fn(self.eng[eng])
        if sig:
            key = "e_" + eng
            self.sem(key)
            self.cnt[key] += 1
            ins.then_inc(self.sems[key], 1)
            ev = (key, self.cnt[key])
            self.last[key] = ev
            return ev
        return None

    def dma(self, q, out, in_, stream, waits=()):
        self.wait(q, waits)
        key = "d_" + stream
        self.sem(key)
        self.cnt[key] += 16
        self.eng[q].dma_start(out=out, in_=in_).then_inc(self.sems[key], 16)
        ev = (key, self.cnt[key])
        self.last[key] = ev
        return ev

    def all_events(self):
        return [v for k, v in self.last.items() if not k.startswith("d_cast_")]

    def barrier(self):
        evs = self.all_events()
        for e in self.eng:
            self.wait(e, evs)

    def bank(self):
        b = self.bank_i
        self.bank_i = (self.bank_i + 1) % 8
        return b, self.banks[b], self.bank_free[b]

    def bank_release(self, b, ev):
        self.bank_free[b] = ev


class Rot:
    def __init__(self, tiles):
        self.tiles = tiles
        self.free = [None] * len(tiles)
        self.i = 0

    def get(self):
        i = self.i
        self.i = (self.i + 1) % len(self.tiles)
        return i, self.tiles[i], self.free[i]

    def release(self, i, ev):
        self.free[i] = ev


def build_nc(n_layers=NL, final=True, dump_xt=False):
    nc = bass.Bass("TRN2", target_bir_lowering=False)
    g = Gen(nc)

    _uid = [0]

    def sbt(name, shape, dt):
        _uid[0] += 1
        return nc.sbuf_tensor("sb%d_%s" % (_uid[0], name), shape, dt)

    def din(name, shape, dt=F32):
        return nc.dram_tensor(name, list(shape), dt, kind="ExternalInput").ap()

    def dscr(name, shape, dt):
        return nc.dram_tensor(name, list(shape), dt).ap()

    x_in = din("x", [NTOK, D])
    cT_in = din("cT", [128, KC, NSLOT])
    pen_in = din("pen", [128, 9])
    bfly_in = din("bfly", [64, 3, 128, 128], BF16)
    fg_in = din("fg", [2, 128, 128], BF16)
    w_ada = din("w_ada", [NL * D, 6 * D])
    b_adaT = din("b_adaT", [128, NL, 96])
    n1T = din("n1T", [128, NL, KC])
    n2T = din("n2T", [128, NL, KC])
    w_qkv = din("w_qkv", [2 * D, QKVW])
    w_o = din("w_o", [2 * AW, D])
    w_f = din("w_f", [2 * D, D])
    bfT = din("bfT", [128, 2, KC])
    w1 = din("w1", [NL * D, DFF])
    b1T = din("b1T", [128, NL, 64])
    w2 = din("w2", [NL * DFF, D])
    b2T = din("b2T", [128, NL, KC])
    fgT = din("fgT", [128, KC])
    ident_in = din("ident", [128, 128])
    cs_in = din("cstab", [128, 2, 512], BF16)
    fr_in = din("frtab", [D, D], BF16)
    fi_in = din("fitab", [D, D], BF16)
    abias_in = din("abias", [3, 2, 128, 384])
    selh_in = din("selh", [6, 6, 128])

    y_out = nc.dram_tensor("y", [NTOK, D], F32, kind="ExternalOutput").ap()
    xt_dump = nc.dram_tensor("xt_dump", [D, NTOK], F32, kind="ExternalOutput").ap() if dump_xt else None
    yy_dump = nc.dram_tensor("yy_dump", [NSLOT * SLOT, D], BF16, kind="ExternalOutput").ap() if dump_xt else None
    dbg_mx = nc.dram_tensor("dbg_mx", [128, 16, 512], BF16, kind="ExternalOutput").ap() if dump_xt else None
    dbg_p = nc.dram_tensor("dbg_p", [128, D], BF16, kind="ExternalOutput").ap() if dump_xt else None
    dbg_s = nc.dram_tensor("dbg_s", [128, D], BF16, kind="ExternalOutput").ap() if dump_xt else None

    XT = dscr("XT", [D, NTOK], F32)
    XTv = XT.rearrange("(kc p) t -> p kc t", p=128)
    XT4 = XT.rearrange("(kc p) (a b) -> p kc a b", p=128, a=NSLOT)
    WQKVb = dscr("WQKVb", [2 * D, QKVW], BF16)
    WOb = dscr("WOb", [2 * AW, D], BF16)
    WFb = dscr("WFb", [2 * D, D], BF16)
    W1b = dscr("W1b", [NL * D, DFF], BF16)
    W2b = dscr("W2b", [NL * DFF, D], BF16)
    QT = dscr("QT", [18, 128, NTOK], BF16)
    KT = dscr("KT", [18, 128, NTOK + 2 * PADT], BF16)
    VV = dscr("VV", [NTOK + 2 * PADT, AW], BF16)
    UT = dscr("UT", [18, 128, NTOK], BF16)
    DS = dscr("DS", [3, 6, NTOK], F32)
    ZZ = dscr("ZZ", [NSLOT, SLOT, 2 * D], BF16)
    YY = dscr("YY", [NSLOT, SLOT, D], BF16)

    with ExitStack() as _stk:
        pb0 = _stk.enter_context(nc.psum_tensor("pb0", [128, 512], F32))
        pb1 = _stk.enter_context(nc.psum_tensor("pb1", [128, 512], F32))
        pb2 = _stk.enter_context(nc.psum_tensor("pb2", [128, 512], F32))
        pb3 = _stk.enter_context(nc.psum_tensor("pb3", [128, 512], F32))
        pb4 = _stk.enter_context(nc.psum_tensor("pb4", [128, 512], F32))
        pb5 = _stk.enter_context(nc.psum_tensor("pb5", [128, 512], F32))
        pb6 = _stk.enter_context(nc.psum_tensor("pb6", [128, 512], F32))
        pb7 = _stk.enter_context(nc.psum_tensor("pb7", [128, 512], F32))
        ones_bf = _stk.enter_context(sbt("ones", [128, 128], BF16))
        ident = _stk.enter_context(sbt("ident", [128, 128], F32))
        MODT = _stk.enter_context(sbt("MODT", [128, NL, 96, NSLOT], F32))
        GS1 = _stk.enter_context(sbt("GS1", [128, NL, KC, NSLOT], F32))
        GS2 = _stk.enter_context(sbt("GS2", [128, NL, KC, NSLOT], F32))
        GB = _stk.enter_context(sbt("GB", [128, NL, KC, NSLOT], F32))
        G2B2 = _stk.enter_context(sbt("G2B2", [128, NL, KC, NSLOT], F32))
        b1s = _stk.enter_context(sbt("b1s", [128, NL, 64], F32))
        fgs = _stk.enter_context(sbt("fgs", [128, KC], F32))
        pens = _stk.enter_context(sbt("pens", [128, 9], F32))
        g.banks = [pb0, pb1, pb2, pb3, pb4, pb5, pb6, pb7]

        ev_c = []
        ev_c.append(g.op("dve", lambda e: e.memset(ones_bf[:], 1.0)))
        ev_c.append(g.dma("sp", ident[:], ident_in, "const"))
        ev_c.append(g.dma("sp", b1s[:], b1T, "const"))
        ev_c.append(g.dma("sp", fgs[:], fgT, "const"))
        ev_c.append(g.dma("sp", pens[:], pen_in, "const"))

        cast_ev = {}

        def cast(name, dst, src, rows, lo=0):
            evs = []
            r = lo
            while r < lo + rows:
                n = min(128, lo + rows - r)
                evs.append(g.dma("pool", dst[r:r + n, :], src[r:r + n, :], "cast_" + name))
                r += n
            cast_ev[name] = evs[-1]

        def cast_layer(l):
            if l % 2 == 0:
                la = l // 2
                cast("qkv%d" % la, WQKVb, w_qkv, D, la * D)
                cast("wo%d" % la, WOb, w_o, AW, la * AW)
            else:
                lf = l // 2
                cast("wf%d" % lf, WFb, w_f, D, lf * D)
            cast("w1_%d" % l, W1b, w1, D, l * D)
            cast("w2_%d" % l, W2b, w2, DFF, l * DFF)

        with sbt("zpad", [128, 4608], BF16) as zpad:
            ez = g.op("dve", lambda e: e.memset(zpad[:], 0.0))
            evz = []
            for side in (0, 1):
                t0 = 0 if side == 0 else PADT + NTOK
                for hh in range(0, 18, 3):
                    evz.append(g.dma("pool", KT[hh:hh + 3, :, t0:t0 + PADT].rearrange("h e t -> e h t"),
                                     zpad[:, 0:3 * PADT].rearrange("p (h t) -> p h t", h=3), "zpad", [ez]))
                for rr in range(0, PADT, 128):
                    evz.append(g.dma("pool", VV[t0 + rr:t0 + rr + 128, :], zpad[:, 0:AW], "zpad", [ez]))
            g.barrier()
        pad_ev = evz[-1]

        for l in range(min(n_layers, NL)):
            cast_layer(l)

        with ExitStack() as _stk:
            p0x0 = _stk.enter_context(sbt("p0x0", [128, 4, D], F32))
            p0x1 = _stk.enter_context(sbt("p0x1", [128, 4, D], F32))
            p0t0 = _stk.enter_context(sbt("p0t0", [128, KC, 512], F32))
            p0t1 = _stk.enter_context(sbt("p0t1", [128, KC, 512], F32))
            xin = Rot([p0x0, p0x1])
            xtr = Rot([p0t0, p0t1])
            xv = x_in.rearrange("(t s p) f -> t p s f", s=4, p=128)
            p0_out = []
            for t in range(NTOK // 512):
                ii, xi, xfree = xin.get()
                ld = g.dma("sp", xi[:], xv[t], "L0_%d" % ii, [xfree])
                oi, xo, ofree = xtr.get()
                last_pe = None
                evs_ev = []
                for kc in range(KC):
                    b, ps, bfree = g.bank()
                    for s in range(4):
                        last_pe = g.op(
                            "pe", lambda e, ps=ps, xi=xi, s=s, kc=kc: e.transpose(
                                ps[:, s * 128:(s + 1) * 128], xi[:, s, kc * 128:(kc + 1) * 128], ident[:]),
                            [ld, bfree, ev_c] if s == 0 else (), sig=(s == 3))
                    eng = "act" if kc % 2 == 0 else "dve"
                    if eng == "act":
                        ev = g.op("act", lambda e, xo=xo, ps=ps, kc=kc: e.copy(xo[:, kc, :], ps[:]), [last_pe, ofree])
                    else:
                        ev = g.op("dve", lambda e, xo=xo, ps=ps, kc=kc: e.tensor_copy(xo[:, kc, :], ps[:]), [last_pe, ofree])
                    g.bank_release(b, ev)
                    evs_ev.append(ev)
                xin.release(ii, last_pe)
                st = g.dma("sp", XTv[:, :, t * 512:(t + 1) * 512], xo[:], "S0_%d" % oi, evs_ev)
                xtr.release(oi, st)
                p0_out.append(st)
        g.barrier()

        with ExitStack() as _stk:
            cact = _stk.enter_context(sbt("cact", [128, KC, NSLOT], F32))
            paw0 = _stk.enter_context(sbt("paw0", [128, KC, 512], F32))
            paw1 = _stk.enter_context(sbt("paw1", [128, KC, 512], F32))
            badas = _stk.enter_context(sbt("badas", [128, NL, 96], F32))
            n1s = _stk.enter_context(sbt("n1s", [128, NL, KC], F32))
            n2s = _stk.enter_context(sbt("n2s", [128, NL, KC], F32))
            bfs = _stk.enter_context(sbt("bfs", [128, 2, KC], F32))
            b2s = _stk.enter_context(sbt("b2s", [128, NL, KC], F32))
            e0 = g.dma("sp", cact[:], cT_in, "L2_0")
            e1 = [g.dma("sp", badas[:], b_adaT, "L2_1"), g.dma("sp", n1s[:], n1T, "L2_1"),
                  g.dma("sp", n2s[:], n2T, "L2_1"), g.dma("sp", bfs[:], bfT, "L2_1"), g.dma("sp", b2s[:], b2T, "L2_1")]
            esil = g.op("act", lambda e: e.activation(cact[:], cact[:], AF.Silu), [e0])
            wrot = Rot([paw0, paw1])
            wav = w_ada.rearrange("(l kc p) c -> l p kc c", l=NL, p=128)
            mod_ev = []
            for l in range(NL):
                for blk in range(24):
                    wi, wt, wfree = wrot.get()
                    ld = g.dma("sp", wt[:], wav[l][:, :, blk * 512:(blk + 1) * 512], "L0_%d" % wi, [wfree])
                    b, ps, bfree = g.bank()
                    lastmm = None
                    for mm in range(4):
                        for kc in range(KC):
                            lastmm = g.op(
                                "pe", lambda e, ps=ps, wt=wt, mm=mm, kc=kc: e.matmul(
                                    ps[:, mm * 4:(mm + 1) * 4], wt[:, kc, mm * 128:(mm + 1) * 128], cact[:, kc, :],
                                    start=(kc == 0), stop=(kc == KC - 1)),
                                [ld, esil, bfree] if (mm == 0 and kc == 0) else (), sig=(mm == 3 and kc == KC - 1))
                    wrot.release(wi, lastmm)
                    ev = g.op("dve", lambda e, ps=ps, l=l, blk=blk: e.tensor_copy(
                        MODT[:, l, blk * 4:(blk + 1) * 4, :], ps[:, 0:16].rearrange("p (m s) -> p m s", m=4)), [lastmm])
                    g.bank_release(b, ev)
                    mod_ev.append(ev)
            fin = []
            for l in range(NL):
                for s in range(NSLOT):
                    ev = g.op("dve", lambda e, l=l, s=s: e.tensor_tensor(
                        MODT[:, l, :, s], MODT[:, l, :, s], badas[:, l, :], ALU.add), [mod_ev, e1])
                    ev = g.op("dve", lambda e, l=l, s=s: e.scalar_tensor_tensor(
                        GS1[:, l, :, s], MODT[:, l, 16:32, s], 1.0, n1s[:, l, :], ALU.add, ALU.mult), [ev])
                    ev2 = g.op("dve", lambda e, l=l, s=s: e.scalar_tensor_tensor(
                        GS2[:, l, :, s], MODT[:, l, 64:80, s], 1.0, n2s[:, l, :], ALU.add, ALU.mult), [ev])
                    ev3 = g.op("dve", lambda e, l=l, s=s: e.tensor_tensor(
                        G2B2[:, l, :, s], MODT[:, l, 80:96, s], b2s[:, l, :], ALU.mult), [ev2])
                    if l % 2 == 1:
                        ev3 = g.op("dve", lambda e, l=l, s=s: e.tensor_tensor(
                            GB[:, l, :, s], MODT[:, l, 32:48, s], bfs[:, l // 2, :], ALU.mult), [ev3])
                    fin.append(ev3)
        g.barrier()

        def col(t, l, kc, s):
            return t[:, l, kc, s:s + 1]

        def modc(l, which, kc, s):
            return MODT[:, l, which * 16 + kc, s:s + 1]

        def norm_stats(xt, sqrot, x_ev, rstd, rstd_free):
            b, ps, bfree = g.bank()
            last = None
            for kc in range(KC):
                si, sq, sfree = sqrot.get()
                es = g.op("act", lambda e, sq=sq, kc=kc: e.activation(sq[:], xt[:, kc, :], AF.Square), [x_ev, sfree])
                last = g.op("pe", lambda e, ps=ps, sq=sq, kc=kc: e.matmul(
                    ps[:], ones_bf[:], sq[:], start=(kc == 0), stop=(kc == KC - 1)),
                    [es, bfree, ev_c] if kc == 0 else [es], sig=True)
                sqrot.release(si, last)
            e1_ = g.op("act", lambda e: e.activation(rstd[:], ps[:], AF.Sqrt, bias=EPS, scale=1.0 / D), [last, rstd_free])
            e2_ = g.op("dve", lambda e: e.reciprocal(rstd[:], rstd[:]), [e1_])
            g.bank_release(b, e1_)
            return e2_

        def ada_apply(xt, h, rstd, rstd_ev, tmprot, x_ev, h_free, gs_t, sh_fn, l, slot_of_col, remap=False):
            evs = []
            for kc in range(KC):
                ti, tmp, tfree = tmprot.get()
                e1_ = g.op("dve", lambda e, tmp=tmp, kc=kc: e.tensor_tensor(tmp[:], xt[:, kc, :], rstd[:], ALU.mult),
                           [x_ev, rstd_ev, tfree])
                ev = None
                for (c0, c1, s) in slot_of_col:
                    if remap:
                        o_ap = h[:, kc, :].rearrange("p (q a b) -> p q a b", q=4, a=4)[:, :, s, :]
                        i_ap = tmp[:, c0:c1].rearrange("p (q b) -> p q b", q=4)
                    else:
                        o_ap = h[:, kc, c0:c1]
                        i_ap = tmp[:, c0:c1]
                    ev = g.op("act", lambda e, o_ap=o_ap, i_ap=i_ap, kc=kc, s=s: e.activation(
                        o_ap, i_ap, AF.Identity, bias=sh_fn(kc, s), scale=col(gs_t, l, kc, s)),
                        [e1_, h_free])
                tmprot.release(ti, ev)
                evs.append(ev)
            return evs

        def phase_A1(l):
            la = l // 2
            with ExitStack() as _stk:
                a1x0 = _stk.enter_context(sbt("a1x0", [128, KC, 512], F32))
                a1x1 = _stk.enter_context(sbt("a1x1", [128, KC, 512], F32))
                H = _stk.enter_context(sbt("a1h", [128, KC, 1024], BF16))
                a1w0 = _stk.enter_context(sbt("a1w0", [128, KC, 768], BF16))
                a1w1 = _stk.enter_context(sbt("a1w1", [128, KC, 768], BF16))
                a1q0 = _stk.enter_context(sbt("a1q0", [128, 6, 512], BF16))
                a1q1 = _stk.enter_context(sbt("a1q1", [128, 6, 512], BF16))
                a1v0 = _stk.enter_context(sbt("a1v0", [128, 768], BF16))
                a1v1 = _stk.enter_context(sbt("a1v1", [128, 768], BF16))
                sq0 = _stk.enter_context(sbt("a1sq0", [128, 512], BF16))
                sq1 = _stk.enter_context(sbt("a1sq1", [128, 512], BF16))
                tm0 = _stk.enter_context(sbt("a1tm0", [128, 512], F32))
                tm1 = _stk.enter_context(sbt("a1tm1", [128, 512], F32))
                rstd = _stk.enter_context(sbt("a1rs", [128, 512], F32))
                xrot = Rot([a1x0, a1x1]); wrot = Rot([a1w0, a1w1]); qrot = Rot([a1q0, a1q1]); vrot = Rot([a1v0, a1v1])
                sqrot = Rot([sq0, sq1]); tmrot = Rot([tm0, tm1])
                wv = WQKVb.rearrange("(l kc p) c -> l p kc c", l=2, p=128)[la]
                rstd_free = None
                h_free = None
                for hb in range(NTOK // 1024):
                    slot = hb // 2
                    h_evs = []
                    for tq in range(2):
                        t0 = hb * 1024 + tq * 512
                        xi, xt, xfree = xrot.get()
                        ld = g.dma("sp", xt[:], XTv[:, :, t0:t0 + 512], "L0_%d" % xi, [xfree])
                        rs_ev = norm_stats(xt, sqrot, ld, rstd, rstd_free)
                        hv = H[:, :, tq * 512:(tq + 1) * 512]
                        evs = ada_apply(xt, hv, rstd, rs_ev, tmrot, ld, h_free, GS1,
                                        lambda kc, s: modc(l, 0, kc, s), l, [(0, 512, slot)])
                        xrot.release(xi, evs[-1])
                        rstd_free = evs[-1]
                        h_evs.append(evs)
                    h_read = []
                    for blk in range(9):
                        gi, which = blk // 3, blk % 3
                        wi, wt, wfree = wrot.get()
                        c0 = gi * 2304 + which * 768
                        ld = g.dma("sp", wt[:], wv[:, :, c0:c0 + 768], "L1_%d" % wi, [wfree, cast_ev["qkv%d" % la]])
                        lastmm = None
                        if which < 2:
                            dst = QT if which == 0 else KT
                            toff = 0 if which == 0 else PADT
                            for tq in range(2):
                                t0 = hb * 1024 + tq * 512
                                qi, qs, qfree = qrot.get()
                                evq = []
                                for hh in range(6):
                                    b, ps, bfree = g.bank()
                                    for kc in range(KC):
                                        lastmm = g.op("pe", lambda e, ps=ps, wt=wt, hh=hh, kc=kc, tq=tq: e.matmul(
                                            ps[:], wt[:, kc, hh * 128:(hh + 1) * 128], H[:, kc, tq * 512:(tq + 1) * 512],
                                            start=(kc == 0), stop=(kc == KC - 1)),
                                            [ld, bfree, h_evs[tq]] if kc == 0 else (), sig=(kc == KC - 1))
                                    eng = "act" if hh % 2 == 0 else "dve"
                                    if eng == "act":
                                        ev = g.op("act", lambda e, qs=qs, ps=ps, hh=hh: e.copy(qs[:, hh, :], ps[:]), [lastmm, qfree])
                                    else:
                                        ev = g.op("dve", lambda e, qs=qs, ps=ps, hh=hh: e.tensor_copy(qs[:, hh, :], ps[:]), [lastmm, qfree])
                                    g.bank_release(b, ev)
                                    evq.append(ev)
                                st = g.dma("sp", dst[gi * 6:(gi + 1) * 6, :, toff + t0:toff + t0 + 512].rearrange("h e t -> e h t"),
                                           qs[:], "S0_%d" % qi, evq)
                                qrot.release(qi, st)
                        else:
                            for sub in range(8):
                                t0 = hb * 1024 + sub * 128
                                vi, vs, vfree = vrot.get()
                                evv = []
                                for (cc0, cc1) in ((0, 512), (512, 768)):
                                    b, ps, bfree = g.bank()
                                    for kc in range(KC):
                                        lastmm = g.op("pe", lambda e, ps=ps, wt=wt, kc=kc, sub=sub, cc0=cc0, cc1=cc1: e.matmul(
                                            ps[:, 0:cc1 - cc0], H[:, kc, sub * 128:(sub + 1) * 128], wt[:, kc, cc0:cc1],
                                            start=(kc == 0), stop=(kc == KC - 1)),
                                            [ld, bfree, h_evs] if kc == 0 else (), sig=(kc == KC - 1))
                                    if cc0 == 0:
                                        ev = g.op("act", lambda e, vs=vs, ps=ps: e.copy(vs[:, 0:512], ps[:, 0:512]), [lastmm, vfree])
                                    else:
                                        ev = g.op("dve", lambda e, vs=vs, ps=ps: e.tensor_copy(vs[:, 512:768], ps[:, 0:256]), [lastmm, vfree])
                                    g.bank_release(b, ev)
                                    evv.append(ev)
                                st = g.dma("sp", VV[PADT + t0:PADT + t0 + 128, gi * 768:(gi + 1) * 768], vs[:], "S1_%d" % vi, evv)
                                vrot.release(vi, st)
                        wrot.release(wi, lastmm)
                        h_read.append(lastmm)
                    h_free = h_read
            g.barrier()

        def phase_A2(l):
            with ExitStack() as _stk:
                abias = _stk.enter_context(sbt("abias", [128, 3, 2, 384], F32))
                a2q0 = _stk.enter_context(sbt("a2q0", [128, 6, 1024], BF16))
                a2q1 = _stk.enter_context(sbt("a2q1", [128, 6, 1024], BF16))
                a2k0 = _stk.enter_context(sbt("a2k0", [128, 6, 3072], BF16))
                a2k1 = _stk.enter_context(sbt("a2k1", [128, 6, 3072], BF16))
                a2u0 = _stk.enter_context(sbt("a2u0", [128, 6, 1024], BF16))
                a2u1 = _stk.enter_context(sbt("a2u1", [128, 6, 1024], BF16))
                a2d0 = _stk.enter_context(sbt("a2d0", [1, 6, 1024], F32))
                a2d1 = _stk.enter_context(sbt("a2d1", [1, 6, 1024], F32))
                a2v0 = _stk.enter_context(sbt("a2v0", [128, 2, 768], BF16))
                a2v1 = _stk.enter_context(sbt("a2v1", [128, 2, 768], BF16))
                a2v2 = _stk.enter_context(sbt("a2v2", [128, 2, 768], BF16))
                a2t0 = _stk.enter_context(sbt("a2t0", [128, 2, 384], F32))
                a2t1 = _stk.enter_context(sbt("a2t1", [128, 2, 384], F32))
                a2p0 = _stk.enter_context(sbt("a2p0", [128, 2, 384], BF16))
                a2p1 = _stk.enter_context(sbt("a2p1", [128, 2, 384], BF16))
                eab = g.dma("sp", abias[:], abias_in.rearrange("g t p c -> p g t c"), "L2_0")
                qrot = Rot([a2q0, a2q1]); krot = Rot([a2k0, a2k1]); urot = Rot([a2u0, a2u1]); drot = Rot([a2d0, a2d1])
                vrot = Rot([a2v0, a2v1, a2v2]); trot = Rot([a2t0, a2t1]); prot = Rot([a2p0, a2p1])
                scale = 128.0 ** -0.5
                for mt in range(NTOK // 1024):
                    P0 = mt * 1024
                    for gi in range(3):
                        d = DIL[gi]
                        blk = 64 * d
                        halo = blk
                        qi, qs, qfree = qrot.get()
                        ki, ks, kfree = krot.get()
                        ldq = g.dma("sp", qs[:], QT[gi * 6:(gi + 1) * 6, :, P0:P0 + 1024].rearrange("h e t -> e h t"), "L0_%d" % qi, [qfree])
                        kw = 1024 + 2 * halo
                        ldk = g.dma("sp", ks[:, :, 0:kw],
                                    KT[gi * 6:(gi + 1) * 6, :, PADT + P0 - halo:PADT + P0 + 1024 + halo].rearrange("h e t -> e h t"),
                                    "L1_%d" % ki, [kfree, pad_ev])
                        ui, us, ufree = urot.get()
                        di, dst_, dfree = drot.get()
                        last_reads = []
                        outs_ev = []
                        for bi in range(1024 // blk):
                            for r in range(d):
                                q0 = bi * blk + r
                                pos0 = P0 + bi * blk
                                colA = 0
                                colB = 0
                                if pos0 % SLOT == 0:
                                    colA = 1 + pos0 // SLOT
                                if (pos0 + blk) % SLOT == 0:
                                    colB = 5 + pos0 // SLOT
                                vi, vt, vfree = vrot.get()
                                row0 = PADT + pos0 - blk + r
                                vsrc = VV[:, gi * 768:(gi + 1) * 768]
                                ldv = [g.dma("sp", vt[:, 0, :], vsrc[sl(row0, 128, d), :], "L3_%d" % vi, [vfree, pad_ev]),
                                       g.dma("sp", vt[0:64, 1, :], vsrc[sl(row0 + 128 * d, 64, d), :], "L3_%d" % vi)]
                                bA, psA, fA = g.bank()
                                bB, psB, fB = g.bank()
                                lastS = None
                                for hh in range(6):
                                    kA = ks[:, hh, sl(q0, 128, d)]
                                    kB = ks[:, hh, sl(q0 + 128 * d, 64, d)]
                                    qq = qs[:, hh, sl(q0, 64, d)]
                                    g.op("pe", lambda e, psA=psA, kA=kA, qq=qq, hh=hh: e.matmul(
                                        psA[:, hh * 64:(hh + 1) * 64], kA, qq, start=True, stop=True),
                                        [ldq, ldk, fA, fB] if hh == 0 else (), sig=False)
                                    lastS = g.op("pe", lambda e, psB=psB, kB=kB, qq=qq, hh=hh: e.matmul(
                                        psB[0:64, hh * 64:(hh + 1) * 64], kB, qq, start=True, stop=True), (), sig=(hh == 5))
                                ti, tt, tfree = trot.get()
                                eA = g.op("dve", lambda e, tt=tt, psA=psA, gi=gi: e.scalar_tensor_tensor(
                                    tt[:, 0, :], psA[:, 0:384], scale, abias[:, gi, 0, :], ALU.mult, ALU.add), [lastS, tfree, eab])
                                eB = g.op("dve", lambda e, tt=tt, psB=psB, gi=gi: e.scalar_tensor_tensor(
                                    tt[0:64, 1, :], psB[0:64, 0:384], scale, abias[0:64, gi, 1, :], ALU.mult, ALU.add), [lastS])
                                g.bank_release(bA, eA)
                                g.bank_release(bB, eB)
                                pi, pt, pfree = prot.get()
                                xA = g.op("act", lambda e, pt=pt, tt=tt, colA=colA: e.activation(
                                    pt[:, 0, :], tt[:, 0, :], AF.Exp, bias=pens[:, colA:colA + 1], scale=1.0), [eA, pfree, ev_c])
                                xB = g.op("act", lambda e, pt=pt, tt=tt, colB=colB: e.activation(
                                    pt[0:64, 1, :], tt[0:64, 1, :], AF.Exp, bias=pens[0:64, colB:colB + 1], scale=1.0), [eB])
                                trot.release(ti, xB)
                                bU, psU, fU = g.bank()
                                bD, psD, fD = g.bank()
                                lastU = None
                                for hh in range(6):
                                    g.op("pe", lambda e, psU=psU, vt=vt, pt=pt, hh=hh: e.matmul(
                                        psU[:, hh * 64:(hh + 1) * 64], vt[:, 0, hh * 128:(hh + 1) * 128], pt[:, 0, hh * 64:(hh + 1) * 64],
                                        start=True, stop=False), [xA, xB, ldv, fU, fD] if hh == 0 else (), sig=False)
                                    g.op("pe", lambda e, psU=psU, vt=vt, pt=pt, hh=hh: e.matmul(
                                        psU[:, hh * 64:(hh + 1) * 64], vt[0:64, 1, hh * 128:(hh + 1) * 128], pt[0:64, 1, hh * 64:(hh + 1) * 64],
                                        start=False, stop=True), (), sig=False)
                                g.op("pe", lambda e, psD=psD, pt=pt: e.matmul(
                                    psD[:, 0:384], ones_bf[:, :], pt[:, 0, :], start=True, stop=False), (), sig=False)
                                lastU = g.op("pe", lambda e, psD=psD, pt=pt: e.matmul(
                                    psD[:, 0:384], ones_bf[0:64, :], pt[0:64, 1, :], start=False, stop=True), (), sig=True)
                                vrot.release(vi, lastU)
                                prot.release(pi, lastU)
                                eU = g.op("act", lambda e, us=us, psU=psU, q0=q0, d=d: e.copy(
                                    us[:, :, sl(q0, 64, d)], psU[:, 0:384].rearrange("p (h q) -> p h q", h=6)), [lastU, ufree])
                                eD = g.op("dve", lambda e, dst_=dst_, psD=psD, q0=q0, d=d: e.tensor_copy(
                                    dst_[0:1, :, sl(q0, 64, d)], psD[0:1, 0:384].rearrange("p (h q) -> p h q", h=6)), [lastU, dfree])
                                g.bank_release(bU, eU)
                                g.bank_release(bD, eD)
                                last_reads.append(lastS)
                                outs_ev += [eU, eD]
                        qrot.release(qi, last_reads[-1])
                        krot.release(ki, last_reads[-1])
                        stu = g.dma("sp", UT[gi * 6:(gi + 1) * 6, :, P0:P0 + 1024].rearrange("h e t -> e h t"), us[:], "S0_%d" % ui, outs_ev)
                        std = g.dma("sp", DS[gi:gi + 1, :, P0:P0 + 1024], dst_[:], "S1_%d" % di, outs_ev)
                        urot.release(ui, stu)
                        drot.release(di, std)
            g.barrier()

        def phase_tail(l, kind, is_last):
            la = l // 2
            KCM = 18 if kind == "attn" else 16
            with ExitStack() as _stk:
                Y = _stk.enter_context(sbt("tY", [128, KC, 512], F32))
                Mx = _stk.enter_context(sbt("tM", [128, KCM, 512], BF16))
                H2 = _stk.enter_context(sbt("tH", [128, KC, 512], BF16))
                wp0 = _stk.enter_context(sbt("tWp0", [128, KCM, 256], BF16))
                wp1 = _stk.enter_context(sbt("tWp1", [128, KCM, 256], BF16))
                w1a = _stk.enter_context(sbt("tW1a", [128, KC, 512], BF16))
                w1b_ = _stk.enter_context(sbt("tW1b", [128, KC, 512], BF16))
                w2a = _stk.enter_context(sbt("tW2a", [128, 4, D], BF16))
                w2b_ = _stk.enter_context(sbt("tW2b", [128, 4, D], BF16))
                u0 = _stk.enter_context(sbt("tU0", [128, 4, 512], BF16))
                u1 = _stk.enter_context(sbt("tU1", [128, 4, 512], BF16))
                r0 = _stk.enter_context(sbt("tR0", [128, 512], F32))
                r1 = _stk.enter_context(sbt("tR1", [128, 512], F32))
                sq0 = _stk.enter_context(sbt("tsq0", [128, 512], BF16))
                sq1 = _stk.enter_context(sbt("tsq1", [128, 512], BF16))
                tm0 = _stk.enter_context(sbt("ttm0", [128, 512], F32))
                tm1 = _stk.enter_context(sbt("ttm1", [128, 512], F32))
                rstd = _stk.enter_context(sbt("trs", [128, 512], F32))
                dsl = rcp = selh = gp0 = gp1 = gs0 = gs1 = fgm = ot0 = ot1 = None
                if kind == "attn":
                    dsl = _stk.enter_context(sbt("tds", [6, 3, 512], F32))
                    rcp = _stk.enter_context(sbt("trc", [6, 512], F32))
                    selh = _stk.enter_context(sbt("tsel", [6, 6, 128], F32))
                else:
                    gp0 = _stk.enter_context(sbt("tg0", [128, D], BF16))
                    gp1 = _stk.enter_context(sbt("tg1", [128, D], BF16))
                    gs0 = _stk.enter_context(sbt("tg2", [128, D], BF16))
                    gs1 = _stk.enter_context(sbt("tg3", [128, D], BF16))
                    fgm = _stk.enter_context(sbt("tfg", [128, 2, 128], BF16))
                if is_last:
                    ot0 = _stk.enter_context(sbt("tO0", [128, 1024], F32))
                    ot1 = _stk.enter_context(sbt("tO1", [128, 1024], F32))
                if kind == "attn":
                    esel = [g.dma("sp", selh[:], selh_in, "L2_0")]
                else:
                    esel = [g.dma("sp", fgm[:], fg_in.rearrange("a p n -> p a n"), "L2_0")]
                wprot = Rot([wp0, wp1]); w1rot = Rot([w1a, w1b_]); w2rot = Rot([w2a, w2b_]); urot = Rot([u0, u1])
                rrot = Rot([r0, r1]); sqrot = Rot([sq0, sq1]); tmrot = Rot([tm0, tm1]); orot = Rot([ot0, ot1]) if is_last else None
                gprot = Rot([gp0, gp1]); gsrot = Rot([gs0, gs1])
                if kind == "attn":
                    wpv = WOb.rearrange("(l kc p) c -> l p kc c", l=2, p=128)[la]
                    wp_cast = cast_ev["wo%d" % la]
                else:
                    wpv = WFb.rearrange("(l kc p) c -> l p kc c", l=2, p=128)[la]
                    wp_cast = cast_ev["wf%d" % la]
                w1v = W1b.rearrange("(l kc p) c -> l p kc c", l=NL, p=128)[l]
                w2v = W2b.rearrange("(l j fc p) c -> l j p fc c", l=NL, j=16, p=128)[l]
                y_free = None; m_free = None; h_free = None; rstd_free = None; ds_free = None; rc_free = None
                for t in range(NTOK // 512):
                    t0 = t * 512
                    slot = t0 // SLOT
                    ldx = g.dma("sp", Y[:], XTv[:, :, t0:t0 + 512], "L4_0", [y_free])
                    if kind == "attn":
                        ldm = g.dma("sp", Mx[:], UT[:, :, t0:t0 + 512].rearrange("h e t -> e h t"), "L4_1", [m_free])
                        ldd = g.dma("sp", dsl[:], DS[:, :, t0:t0 + 512].rearrange("g h t -> h g t"), "L4_2", [ds_free])
                        ea = g.op("dve", lambda e: e.tensor_tensor(rcp[:], dsl[:, 0, :], dsl[:, 1, :], ALU.add), [ldd, rc_free])
                        ea = g.op("dve", lambda e: e.tensor_tensor(rcp[:], rcp[:], dsl[:, 2, :], ALU.add), [ea])
                        ds_free = ea
                        ea = g.op("dve", lambda e: e.reciprocal(rcp[:], rcp[:]), [ea])
                        m_evs = []
                        lastb = None
                        for hh in range(6):
                            b, ps, bfree = g.bank()
                            lastb = g.op("pe", lambda e, ps=ps, hh=hh: e.matmul(ps[:], selh[:, hh, :], rcp[:], start=True, stop=True),
                                         [ea, esel, bfree])
                            ev = None
                            for gi in range(3):
                                hd = gi * 6 + hh
                                ev = g.op("dve", lambda e, ps=ps, hd=hd: e.tensor_tensor(Mx[:, hd, :], Mx[:, hd, :], ps[:], ALU.mult),
                                          [lastb, ldm])
                                m_evs.append(ev)
                            g.bank_release(b, ev)
                        rc_free = lastb
                    else:
                        c_ = slot
                        tq = (t0 % SLOT) // 512
                        m_evs = []
                        for s in range(4):
                            pi, pt, pfree = gprot.get()
                            si, st_, sfree = gsrot.get()
                            lds = g.dma("sp", st_[:], YY[c_, tq * 512 + s * 128:tq * 512 + (s + 1) * 128, :], "L1_%d" % si, [sfree])
                            ldp = None
                            r0_ = 512 * c_ + 128 * tq + 32 * s
                            for k1 in range(4):
                                ldp = g.dma("sp", pt[32 * k1:32 * (k1 + 1), :], YY[k1, r0_:r0_ + 32, :], "L0_%d" % pi, [pfree] if k1 == 0 else ())
                            last = None
                            for q in range(4):
                                b, ps, bfree = g.bank()
                                for jj in range(4):
                                    kc = q * 4 + jj
                                    g.op("pe", lambda e, ps=ps, pt=pt, kc=kc, jj=jj: e.matmul(
                                        ps[:, jj * 128:(jj + 1) * 128], pt[:, kc * 128:(kc + 1) * 128], fgm[:, 0, :], start=True, stop=False),
                                        [ldp, lds, esel, bfree] if jj == 0 else (), sig=False)
                                    last = g.op("pe", lambda e, ps=ps, st_=st_, kc=kc, jj=jj: e.matmul(
                                        ps[:, jj * 128:(jj + 1) * 128], st_[:, kc * 128:(kc + 1) * 128], fgm[:, 1, :], start=False, stop=True),
                                        (), sig=(jj == 3))
                                dstv = Mx[:, q * 4:(q + 1) * 4, s * 128:(s + 1) * 128]
                                srcv = ps[:].rearrange("p (j n) -> p j n", j=4)
                                if q % 2 == 0:
                                    ev = g.op("act", lambda e, dstv=dstv, srcv=srcv: e.copy(dstv, srcv), [last, m_free])
                                else:
                                    ev = g.op("dve", lambda e, dstv=dstv, srcv=srcv: e.tensor_copy(dstv, srcv), [last, m_free])
                                g.bank_release(b, ev)
                                m_evs.append(ev)
                            gprot.release(pi, last)
                            gsrot.release(si, last)
                    y_evs = [None] * KC
                    lastp = None
                    for mb in range(8):
                        wi, wt, wfree = wprot.get()
                        ldw = g.dma("sp", wt[:], wpv[:, :, mb * 256:(mb + 1) * 256], "L3_%d" % wi, [wfree, wp_cast])
                        for m2 in range(2):
                            m = mb * 2 + m2
                            b, ps, bfree = g.bank()
                            for kc in range(KCM):
                                lastp = g.op("pe", lambda e, ps=ps, wt=wt, kc=kc, m2=m2: e.matmul(
                                    ps[:], wt[:, kc, m2 * 128:(m2 + 1) * 128], Mx[:, kc, :], start=(kc == 0), stop=(kc == KCM - 1)),
                                    [ldw, bfree, m_evs] if kc == 0 else (), sig=(kc == KCM - 1))
                            ev = g.op("dve", lambda e, ps=ps, m=m: e.scalar_tensor_tensor(
                                Y[:, m, :], ps[:], modc(l, 2, m, slot), Y[:, m, :], ALU.mult, ALU.add), [lastp, ldx])
                            g.bank_release(b, ev)
                            if kind == "fft":
                                ev = g.op("dve", lambda e, m=m: e.tensor_scalar(
                                    Y[:, m, :], Y[:, m, :], col(GB, l, m, slot), None, ALU.add), [ev])
                            y_evs[m] = ev
                        wprot.release(wi, lastp)
                    m_free = lastp
                    rs_ev = norm_stats(Y, sqrot, y_evs, rstd, rstd_free)
                    h_evs = ada_apply(Y, H2, rstd, rs_ev, tmrot, y_evs, h_free, GS2,
                                      lambda kc, s: modc(l, 3, kc, s), l, [(0, 512, slot)])
                    rstd_free = h_evs[-1]
                    yb = []
                    for m in range(KC):
                        yb.append(g.op("dve", lambda e, m=m: e.tensor_scalar(
                            Y[:, m, :], Y[:, m, :], col(G2B2, l, m, slot), None, ALU.add), [h_evs[m]]))
                    lastmm = None
                    for j in range(16):
                        i1, w1t, f1 = w1rot.get()
                        i2, w2t, f2 = w2rot.get()
                        ld1 = g.dma("sp", w1t[:], w1v[:, :, j * 512:(j + 1) * 512], "L5_%d" % i1, [f1, cast_ev["w1_%d" % l]])
                        ld2 = g.dma("sp", w2t[:], w2v[j], "L6_%d" % i2, [f2, cast_ev["w2_%d" % l]])
                        ui, ut, ufree = urot.get()
                        u_evs = []
                        last1 = None
                        for fc in range(4):
                            b, ps, bfree = g.bank()
                            for kc in range(KC):
                                last1 = g.op("pe", lambda e, ps=ps, w1t=w1t, kc=kc, fc=fc: e.matmul(
                                    ps[:], w1t[:, kc, fc * 128:(fc + 1) * 128], H2[:, kc, :], start=(kc == 0), stop=(kc == KC - 1)),
                                    [ld1, bfree, h_evs] if kc == 0 else (), sig=(kc == KC - 1))
                            ri, rt, rfree = rrot.get()
                            er = g.op("act", lambda e, rt=rt, ps=ps, j=j, fc=fc: e.activation(
                                rt[:], ps[:], AF.Relu, bias=b1s[:, l, j * 4 + fc:j * 4 + fc + 1], scale=1.0), [last1, rfree, ev_c])
                            g.bank_release(b, er)
                            eu = g.op("dve", lambda e, ut=ut, rt=rt, fc=fc: e.tensor_tensor(ut[:, fc, :], rt[:], rt[:], ALU.mult), [er, ufree])
                            rrot.release(ri, eu)
                            u_evs.append(eu)
                        w1rot.release(i1, last1)
                        for m in range(KC):
                            b, ps, bfree = g.bank()
                            for fc in range(4):
                                lastmm = g.op("pe", lambda e, ps=ps, w2t=w2t, fc=fc, m=m, ut=ut: e.matmul(
                                    ps[:], w2t[:, fc, m * 128:(m + 1) * 128], ut[:, fc, :], start=(fc == 0), stop=(fc == 3)),
                                    [ld2, bfree, u_evs] if fc == 0 else (), sig=(fc == 3))
                            ev = g.op("dve", lambda e, ps=ps, m=m: e.scalar_tensor_tensor(
                                Y[:, m, :], ps[:], modc(l, 5, m, slot), Y[:, m, :], ALU.mult, ALU.add), [lastmm, yb[m]])
                            g.bank_release(b, ev)
                            yb[m] = ev
                        w2rot.release(i2, lastmm)
                        urot.release(ui, lastmm)
                    h_free = lastmm
                    if not is_last:
                        st = g.dma("sp", XTv[:, :, t0:t0 + 512], Y[:], "S0_0", yb)
                        y_free = st
                    else:
                        rs_ev = norm_stats(Y, sqrot, yb, rstd, rstd_free)
                        fe = []
                        for kc in range(KC):
                            e1_ = g.op("dve", lambda e, kc=kc: e.tensor_tensor(Y[:, kc, :], Y[:, kc, :], rstd[:], ALU.mult), [rs_ev, yb[kc]])
                            fe.append(g.op("act", lambda e, kc=kc: e.activation(
                                Y[:, kc, :], Y[:, kc, :], AF.Copy, scale=fgs[:, kc:kc + 1]), [e1_, ev_c]))
                        rstd_free = fe[-1]
                        sts = []
                        lastt = None
                        for sub in range(4):
                            for hf in range(2):
                                oi, ot, ofree = orot.get()
                                oev = []
                                for q2 in range(2):
                                    q = hf * 2 + q2
                                    b, ps, bfree = g.bank()
                                    for jj in range(4):
                                        kc = q * 4 + jj
                                        lastt = g.op("pe", lambda e, ps=ps, kc=kc, jj=jj, sub=sub: e.transpose(
                                            ps[:, jj * 128:(jj + 1) * 128], Y[:, kc, sub * 128:(sub + 1) * 128], ident[:]),
                                            [fe, bfree, ev_c] if jj == 0 else (), sig=(jj == 3))
                                    if q % 2 == 0:
                                        ev = g.op("act", lambda e, ot=ot, ps=ps, q2=q2: e.copy(ot[:, q2 * 512:(q2 + 1) * 512], ps[:]), [lastt, ofree])
                                    else:
                                        ev = g.op("dve", lambda e, ot=ot, ps=ps, q2=q2: e.tensor_copy(ot[:, q2 * 512:(q2 + 1) * 512], ps[:]), [lastt, ofree])
                                    g.bank_release(b, ev)
                                    oev.append(ev)
                                st = g.dma("sp", y_out[t0 + sub * 128:t0 + (sub + 1) * 128, hf * 1024:(hf + 1) * 1024], ot[:], "S1_%d" % oi, oev)
                                orot.release(oi, st)
                                sts.append(st)
                        y_free = [lastt, sts]
            g.barrier()

        def phase_F1(l):
            with ExitStack() as _stk:
                f1x0 = _stk.enter_context(sbt("f1x0", [128, KC, 512], F32))
                f1x1 = _stk.enter_context(sbt("f1x1", [128, KC, 512], F32))
                H = _stk.enter_context(sbt("f1h", [128, KC, 512], BF16))
                cs = _stk.enter_context(sbt("f1cs", [128, 2, 512], BF16))
                f1m0 = _stk.enter_context(sbt("f1m0", [128, 4, 3, 128], BF16))
                f1m1 = _stk.enter_context(sbt("f1m1", [128, 4, 3, 128], BF16))
                f1g0 = _stk.enter_context(sbt("f1g0", [128, 8, 512], BF16))
                f1g1 = _stk.enter_context(sbt("f1g1", [128, 8, 512], BF16))
                f1z0 = _stk.enter_context(sbt("f1z0", [128, 8, 512], BF16))
                f1z1 = _stk.enter_context(sbt("f1z1", [128, 8, 512], BF16))
                sq0 = _stk.enter_context(sbt("f1sq0", [128, 512], BF16))
                sq1 = _stk.enter_context(sbt("f1sq1", [128, 512], BF16))
                tm0 = _stk.enter_context(sbt("f1tm0", [128, 512], F32))
                tm1 = _stk.enter_context(sbt("f1tm1", [128, 512], F32))
                rstd = _stk.enter_context(sbt("f1rs", [128, 512], F32))
                ecs = g.dma("sp", cs[:], cs_in, "L2_0")
                xrot = Rot([f1x0, f1x1]); mrot = Rot([f1m0, f1m1]); grot = Rot([f1g0, f1g1]); zrot = Rot([f1z0, f1z1])
                sqrot = Rot([sq0, sq1]); tmrot = Rot([tm0, tm1])
                rstd_free = None; h_free = None
                ZZv = ZZ.rearrange("k (tb q b) c -> tb q k b c", tb=16, q=4)
                for tb in range(16):
                    xi, xt, xfree = xrot.get()
                    ld = None
                    for a in range(4):
                        ld = g.dma("sp", xt[:, :, a * 128:(a + 1) * 128], XT4[:, :, a, tb * 128:(tb + 1) * 128], "L0_%d" % xi, [xfree] if a == 0 else ())
                    mi, mt_, mfree = mrot.get()
                    ldm = g.dma("sp", mt_[:], bfly_in[tb * 4:(tb + 1) * 4].rearrange("q m k n -> k q m n"), "L1_%d" % mi, [mfree])
                    rs_ev = norm_stats(xt, sqrot, ld, rstd, rstd_free)
                    h_evs = ada_apply(xt, H, rstd, rs_ev, tmrot, ld, h_free, GS1,
                                      lambda kc, s: modc(l, 0, kc, s), l, [(a * 128, (a + 1) * 128, a) for a in range(4)], remap=True)
                    xrot.release(xi, h_evs[-1])
                    rstd_free = h_evs[-1]
                    lastz = None
                    for q in range(4):
                        gi_, gt, gfree = grot.get()
                        g_evs = []
                        lastg = None
                        for gI in range(8):
                            b, ps, bfree = g.bank()
                            for k2 in range(2):
                                kc = gI * 2 + k2
                                lhs = H[:, kc, q * 128:(q + 1) * 128]
                                lastg = g.op("pe", lambda e, ps=ps, lhs=lhs, k2=k2: e.matmul(
                                    ps[:], lhs, cs[:, k2, :], start=(k2 == 0), stop=(k2 == 1)),
                                    [h_evs, ecs, bfree] if k2 == 0 else (), sig=(k2 == 1))
                            if gI % 2 == 0:
                                ev = g.op("act", lambda e, gt=gt, ps=ps, gI=gI: e.copy(gt[:, gI, :], ps[:]), [lastg, gfree])
                            else:
                                ev = g.op("dve", lambda e, gt=gt, ps=ps, gI=gI: e.tensor_copy(gt[:, gI, :], ps[:]), [lastg, gfree])
                            g.bank_release(b, ev)
                            g_evs.append(ev)
                        zi, zt, zfree = zrot.get()
                        z_evs = []
                        for gI in range(8):
                            b, ps, bfree = g.bank()
                            A_ = gt[:, gI, 0:256]
                            B_ = gt[:, gI, 256:512]
                            g.op("pe", lambda e, ps=ps, A_=A_, q=q: e.matmul(ps[:, 0:256], mt_[:, q, 0, :], A_, start=True, stop=False),
                                 [g_evs[gI], ldm, bfree], sig=False)
                            g.op("pe", lambda e, ps=ps, B_=B_, q=q: e.matmul(ps[:, 0:256], mt_[:, q, 1, :], B_, start=False, stop=True), (), sig=False)
                            g.op("pe", lambda e, ps=ps, A_=A_, q=q: e.matmul(ps[:, 256:512], mt_[:, q, 1, :], A_, start=True, stop=False), (), sig=False)
                            lastz = g.op("pe", lambda e, ps=ps, B_=B_, q=q: e.matmul(ps[:, 256:512], mt_[:, q, 2, :], B_, start=False, stop=True), (), sig=True)
                            if gI % 2 == 0:
                                ev = g.op("act", lambda e, zt=zt, ps=ps, gI=gI: e.copy(zt[:, gI, :], ps[:]), [lastz, zfree])
                            else:
                                ev = g.op("dve", lambda e, zt=zt, ps=ps, gI=gI: e.tensor_copy(zt[:, gI, :], ps[:]), [lastz, zfree])
                            g.bank_release(b, ev)
                            z_evs.append(ev)
                        grot.release(gi_, lastz)
                        st = None
                        for k1 in range(4):
                            st = g.dma("sp", ZZv[tb, q, k1], zt[32 * k1:32 * (k1 + 1)].rearrange("p g c -> p (g c)"), "S0_%d" % zi, z_evs if k1 == 0 else ())
                        zrot.release(zi, st)
                    mrot.release(mi, lastz)
                    h_free = lastz
            g.barrier()

        def phase_F2(l):
            with ExitStack() as _stk:
                Fr = _stk.enter_context(sbt("f2fr", [128, KC, 1024], BF16))
                Fi = _stk.enter_context(sbt("f2fi", [128, KC, 1024], BF16))
                f2z0 = _stk.enter_context(sbt("f2z0", [128, KC, 2, 512], BF16))
                f2z1 = _stk.enter_context(sbt("f2z1", [128, KC, 2, 512], BF16))
                f2y0 = _stk.enter_context(sbt("f2y0", [128, 8, 512], BF16))
                f2y1 = _stk.enter_context(sbt("f2y1", [128, 8, 512], BF16))
                zrot = Rot([f2z0, f2z1]); yrot = Rot([f2y0, f2y1])
                frv = fr_in.rearrange("(bc p) k -> p bc k", p=128)
                fiv = fi_in.rearrange("(bc p) k -> p bc k", p=128)
                f_free = None
                for kh in range(2):
                    ldf = [g.dma("sp", Fr[:], frv[:, :, kh * 1024:(kh + 1) * 1024], "L2_0", [f_free]),
                           g.dma("sp", Fi[:], fiv[:, :, kh * 1024:(kh + 1) * 1024], "L2_0")]
                    last = None
                    for k1 in range(NSLOT):
                        zsrc = ZZ[k1].rearrange("(bc p) c -> p bc c", p=128)
                        for ct in range(4):
                            zi, zt, zfree = zrot.get()
                            ldz = g.dma("sp", zt[:].rearrange("p b g c -> p b (g c)"), zsrc[:, :, ct * 1024:(ct + 1) * 1024], "L0_%d" % zi, [zfree])
                            yi, yt, yfree = yrot.get()
                            y_evs = []
                            for kk in range(8):
                                b, ps, bfree = g.bank()
                                for bc in range(KC):
                                    g.op("pe", lambda e, ps=ps, zt=zt, bc=bc, kk=kk: e.matmul(
                                        ps[:].rearrange("p (g c) -> p g c", g=2), Fr[:, bc, kk * 128:(kk + 1) * 128], zt[:, bc, :, 0:256],
                                        start=(bc == 0), stop=False), [ldf, ldz, bfree] if bc == 0 else (), sig=False)
                                    last = g.op("pe", lambda e, ps=ps, zt=zt, bc=bc, kk=kk: e.matmul(
                                        ps[:].rearrange("p (g c) -> p g c", g=2), Fi[:, bc, kk * 128:(kk + 1) * 128], zt[:, bc, :, 256:512],
                                        start=False, stop=(bc == KC - 1)), (), sig=(bc == KC - 1))
                                if kk % 2 == 0:
                                    ev = g.op("act", lambda e, yt=yt, ps=ps, kk=kk: e.copy(yt[:, kk, :], ps[:]), [last, yfree])
                                else:
                                    ev = g.op("dve", lambda e, yt=yt, ps=ps, kk=kk: e.tensor_copy(yt[:, kk, :], ps[:]), [last, yfree])
                                g.bank_release(b, ev)
                                y_evs.append(ev)
                            zrot.release(zi, last)
                            st = g.dma("sp", YY[k1, kh * 1024:(kh + 1) * 1024, ct * 512:(ct + 1) * 512].rearrange("(kk p) c -> p kk c", p=128),
                                       yt[:], "S0_%d" % yi, y_evs)
                            yrot.release(yi, st)
                    f_free = last
            g.barrier()

        for l in range(n_layers):
            is_last = final and (l == n_layers - 1)
            if l % 2 == 0:
                phase_A1(l)
                phase_A2(l)
                phase_tail(l, "attn", is_last)
            else:
                phase_F1(l)
                phase_F2(l)
                phase_tail(l, "fft", is_last)

        if dump_xt:
            with sbt("dmp", [128, KC, 512], F32) as dmp:
                fr = None
                for t in range(NTOK // 512):
                    ld = g.dma("sp", dmp[:], XTv[:, :, t * 512:(t + 1) * 512], "L7_0", [fr])
                    fr = g.dma("sp", xt_dump.rearrange("(kc p) t -> p kc t", p=128)[:, :, t * 512:(t + 1) * 512], dmp[:], "S2_0", [ld])
                g.barrier()
            with sbt("dmpy", [128, 16, D], BF16) as dmpy:
                fr = None
                for k1 in range(NSLOT):
                    ld = g.dma("sp", dmpy[:], YY[k1].rearrange("(r p) c -> p r c", p=128), "L7_0", [fr])
                    fr = g.dma("sp", yy_dump[k1 * SLOT:(k1 + 1) * SLOT, :].rearrange("(r p) c -> p r c", p=128), dmpy[:], "S2_0", [ld])
                g.barrier()
        g.barrier()
    return nc


def _bf(a):
    return np.ascontiguousarray(a.astype(ml_dtypes.bfloat16))


def _tables():
    tabs = {}
    tabs["ident"] = np.eye(128, dtype=np.float32)
    cin = (np.arange(2)[None, :, None] * 128 + np.arange(128)[:, None, None]).astype(np.float64)
    j = np.arange(256)[None, None, :].astype(np.float64)
    ang = 2 * np.pi * ((cin * j) % 256) / 256.0
    tabs["cstab"] = _bf(np.concatenate([np.cos(ang), np.sin(ang)], axis=2) / 16.0)
    b = np.arange(SLOT, dtype=np.int64)
    ang = 2 * np.pi * ((b[:, None] * b[None, :]) % SLOT).astype(np.float64) / SLOT
    tabs["frtab"] = _bf(np.cos(ang) / np.sqrt(SLOT))
    tabs["fitab"] = _bf(np.sin(ang) / np.sqrt(SLOT))
    slopes = np.exp2(-8.0 * np.arange(1, 19, dtype=np.float64) / 18.0)
    ab = np.zeros((3, 2, 128, 384), np.float32)
    kj = np.arange(192) - 64
    qi = np.arange(64)
    du = kj[:, None] - qi[None, :]
    valid = np.abs(du) <= 64
    for gi in range(3):
        for h in range(6):
            v = np.where(valid, -slopes[gi * 6 + h] * np.abs(du) * DIL[gi], NEG).astype(np.float32)
            ab[gi, 0, :, h * 64:(h + 1) * 64] = v[0:128]
            ab[gi, 1, 0:64, h * 64:(h + 1) * 64] = v[128:192]
    tabs["abias"] = ab
    sel = np.zeros((6, 6, 128), np.float32)
    for h in range(6):
        sel[h, h, :] = 1.0
    tabs["selh"] = sel
    return tabs


def _core_tables(is_prompt):
    pen = np.zeros((128, 9), np.float32)
    for s in range(4):
        first_masked = (not is_prompt) or s == 0
        last_masked = (not is_prompt) or s == 3
        if first_masked:
            pen[0:64, 1 + s] = NEG
        if last_masked:
            pen[0:64, 5 + s] = NEG
    bf = np.zeros((64, 3, 128, 128), np.float64)
    a = np.arange(4)
    for G_ in range(64):
        for bp in range(32):
            b = 32 * G_ + bp
            for k1 in range(4):
                if is_prompt:
                    psi = 2 * np.pi * ((b * k1) % 8192) / 8192.0 + 2 * np.pi * ((a * k1) % 4) / 4.0
                    tr = 0.5 * np.cos(psi)
                    ti = -0.5 * np.sin(psi)
                else:
                    tr = (a == k1).astype(np.float64)
                    ti = np.zeros(4)
                bf[G_, 0, a * 32 + bp, k1 * 32 + bp] = tr
                bf[G_, 1, a * 32 + bp, k1 * 32 + bp] = ti
                bf[G_, 2, a * 32 + bp, k1 * 32 + bp] = -tr
    fg = np.zeros((2, 128, 128), np.float32)
    if is_prompt:
        for k1 in range(4):
            for jp in range(32):
                fg[0, 32 * k1 + jp, 4 * jp + k1] = 1.0
    else:
        fg[1] = np.eye(128, dtype=np.float32)
    return pen, _bf(bf), _bf(fg)


def _colT(v, nchunk):
    v = np.asarray(v, np.float32)
    lead = v.shape[:-1]
    return np.ascontiguousarray(np.moveaxis(v.reshape(*lead, nchunk, 128), -1, 0))


def make_in_maps(inputs, cores=None):
    f32 = lambda a: np.ascontiguousarray(np.asarray(a, np.float32))
    xp = f32(inputs["x_prompt"]); xs = f32(inputs["x_sample"])
    cp = f32(inputs["c_prompt"]); cs = f32(inputs["c_sample"])
    shared = dict(_tables())
    shared["w_ada"] = f32(inputs["w_ada"]).reshape(NL * D, 6 * D)
    shared["b_adaT"] = _colT(inputs["b_ada"], 96)
    shared["n1T"] = _colT(inputs["norm1_g"], KC)
    shared["n2T"] = _colT(inputs["norm2_g"], KC)
    shared["w_qkv"] = f32(inputs["w_qkv"]).reshape(2 * D, QKVW)
    shared["w_o"] = f32(inputs["w_o"]).reshape(2 * AW, D)
    shared["w_f"] = f32(inputs["w_f"]).reshape(2 * D, D)
    shared["bfT"] = _colT(inputs["b_f"], KC)
    shared["w1"] = f32(inputs["w1"]).reshape(NL * D, DFF)
    shared["b1T"] = _colT(inputs["b1"], 64)
    shared["w2"] = f32(inputs["w2"]).reshape(NL * DFF, D)
    shared["b2T"] = _colT(inputs["b2"], KC)
    shared["fgT"] = _colT(inputs["final_g"], KC)
    tabs_p = _core_tables(True)
    tabs_s = _core_tables(False)
    maps = []
    for c in (range(N_CORES) if cores is None else cores):
        m = dict(shared)
        if c < 2:
            m["x"] = xp[c]
            cc = np.repeat(cp[c:c + 1], 4, axis=0)
            pen, bfly, fg = tabs_p
        else:
            cs_i = min(c, 5) - 2
            m["x"] = xs[4 * cs_i:4 * cs_i + 4].reshape(NTOK, D)
            cc = cs[4 * cs_i:4 * cs_i + 4]
            pen, bfly, fg = tabs_s
        m["cT"] = np.ascontiguousarray(np.transpose(cc.reshape(4, KC, 128), (2, 1, 0)))
        m["pen"] = pen
        m["bfly"] = bfly
        m["fg"] = fg
        maps.append(m)
    return maps


_NC_CACHE = {}


def kernel(**inputs):
    if "nc" not in _NC_CACHE:
        _NC_CACHE["nc"] = build_nc()
    nc = _NC_CACHE["nc"]
    in_maps = make_in_maps(inputs)
    res = run_bass_kernel_spmd(nc, in_maps, core_ids=list(range(N_CORES)))
    outs = [r["y"] for r in res.results]
    y_prompt = np.stack([outs[0], outs[1]], axis=0).astype(np.float32)
    y_sample = np.concatenate([outs[c].reshape(4, SLOT, D) for c in range(2, 6)], axis=0).astype(np.float32)
    return (y_prompt, y_sample)
```

```python
import numpy as np
from contextlib import ExitStack
import ml_dtypes
import concourse.bass as bass
import concourse.mybir as mybir
from concourse.bass_utils import run_bass_kernel_spmd

F32 = mybir.dt.float32
BF16 = mybir.dt.bfloat16
AF = mybir.ActivationFunctionType
ALU = mybir.AluOpType

D = 2048
KC = 16
NSLOT = 4
SLOT = 2048
NTOK = 8192
DFF = 8192
QKVW = 6912
AW = 2304
NL = 4
PADT = 1024
EPS = 1e-6
NEG = -30000.0
DIL = (1, 4, 16)
N_CORES = 8


def sl(start, n, step=1):
    return slice(start, start + (n - 1) * step + 1, step)


def _flat(evs):
    out = []
    for e in evs:
        if e is None:
            continue
        if isinstance(e, tuple) and len(e) == 2 and isinstance(e[0], str):
            out.append(e)
        else:
            out.extend(_flat(e))
    return out


class Gen:
    def __init__(self, nc):
        self.nc = nc
        self.eng = {"pe": nc.tensor, "act": nc.scalar, "dve": nc.vector, "pool": nc.gpsimd, "sp": nc.sync}
        self.sems = {}
        self.cnt = {}
        self.waited = {e: {} for e in self.eng}
        self.last = {}
        self.bank_free = [None] * 8
        self.bank_i = 0
        self.banks = None

    def sem(self, key):
        if key not in self.sems:
            self.sems[key] = self.nc.alloc_semaphore(key)
            self.cnt[key] = 0
        return self.sems[key]

    def wait(self, eng, evs):
        for key, val in _flat(evs):
            if self.waited[eng].get(key, 0) >= val:
                continue
            self.eng[eng].wait_ge(self.sems[key], val)
            self.waited[eng][key] = val

    def op(self, eng, fn, waits=(), sig=True):
        self.wait(eng, waits)
        ins = fn(self.eng[eng])
        if sig:
            key = "e_" + eng
            self.sem(key)
            self.cnt[key] += 1
            ins.then_inc(self.sems[key], 1)
            ev = (key, self.cnt[key])
            self.last[key] = ev
            return ev
        return None

    def dma(self, q, out, in_, stream, waits=()):
        self.wait(q, waits)
        key = "d_" + stream
        self.sem(key)
        self.cnt[key] += 16
        self.eng[q].dma_start(out=out, in_=in_).then_inc(self.sems[key], 16)
        ev = (key, self.cnt[key])
        self.last[key] = ev
        return ev

    def all_events(self):
        return [v for k, v in self.last.items() if not k.startswith("d_cast_")]

    def barrier(self):
        evs = self.all_events()
        for e in self.eng:
            self.wait(e, evs)

    def bank(self):
        b = self.bank_i
        self.bank_i = (self.bank_i + 1) % 8
        return b, self.banks[b], self.bank_free[b]

    def bank_release(self, b, ev):
        self.bank_free[b] = ev


class Rot:
    def __init__(self, tiles):
        self.tiles = tiles
        self.free = [None] * len(tiles)
        self.i = 0

    def get(self):
        i = self.i
        self.i = (self.i + 1) % len(self.tiles)
        return i, self.tiles[i], self.free[i]

    def release(self, i, ev):
        self.free[i] = ev


def build_nc(n_layers=NL, final=True, dump_xt=False):
    nc = bass.Bass("TRN2", target_bir_lowering=False)
    g = Gen(nc)

    _uid = [0]

    def sbt(name, shape, dt):
        _uid[0] += 1
        return nc.sbuf_tensor("sb%d_%s" % (_uid[0], name), shape, dt)

    def din(name, shape, dt=F32):
        return nc.dram_tensor(name, list(shape), dt, kind="ExternalInput").ap()

    def dscr(name, shape, dt):
        return nc.dram_tensor(name, list(shape), dt).ap()

    x_in = din("x", [NTOK, D])
    cT_in = din("cT", [128, KC, NSLOT])
    pen_in = din("pen", [128, 9])
    bfly_in = din("bfly", [64, 3, 128, 128], BF16)
    fg_in = din("fg", [2, 128, 128], BF16)
    w_ada = din("w_ada", [NL * D, 6 * D])
    b_adaT = din("b_adaT", [128, NL, 96])
    n1T = din("n1T", [128, NL, KC])
    n2T = din("n2T", [128, NL, KC])
    w_qkv = din("w_qkv", [2 * D, QKVW])
    w_o = din("w_o", [2 * AW, D])
    w_f = din("w_f", [2 * D, D])
    bfT = din("bfT", [128, 2, KC])
    w1 = din("w1", [NL * D, DFF])
    b1T = din("b1T", [128, NL, 64])
    w2 = din("w2", [NL * DFF, D])
    b2T = din("b2T", [128, NL, KC])
    fgT = din("fgT", [128, KC])
    ident_in = din("ident", [128, 128])
    cs_in = din("cstab", [128, 2, 512], BF16)
    fr_in = din("frtab", [D, D], BF16)
    fi_in = din("fitab", [D, D], BF16)
    abias_in = din("abias", [3, 2, 128, 384])
    selh_in = din("selh", [6, 6, 128])

    y_out = nc.dram_tensor("y", [NTOK, D], F32, kind="ExternalOutput").ap()
    xt_dump = nc.dram_tensor("xt_dump", [D, NTOK], F32, kind="ExternalOutput").ap() if dump_xt else None
    yy_dump = nc.dram_tensor("yy_dump", [NSLOT * SLOT, D], BF16, kind="ExternalOutput").ap() if dump_xt else None
    dbg_mx = nc.dram_tensor("dbg_mx", [128, 16, 512], BF16, kind="ExternalOutput").ap() if dump_xt else None
    dbg_p = nc.dram_tensor("dbg_p", [128, D], BF16, kind="ExternalOutput").ap() if dump_xt else None
    dbg_s = nc.dram_tensor("dbg_s", [128, D], BF16, kind="ExternalOutput").ap() if dump_xt else None

    XT = dscr("XT", [D, NTOK], F32)
    XTv = XT.rearrange("(kc p) t -> p kc t", p=128)
    XT4 = XT.rearrange("(kc p) (a b) -> p kc a b", p=128, a=NSLOT)
    WQKVb = dscr("WQKVb", [2 * D, QKVW], BF16)
    WOb = dscr("WOb", [2 * AW, D], BF16)
    WFb = dscr("WFb", [2 * D, D], BF16)
    W1b = dscr("W1b", [NL * D, DFF], BF16)
    W2b = dscr("W2b", [NL * DFF, D], BF16)
    QT = dscr("QT", [18, 128, NTOK], BF16)
    KT = dscr("KT", [18, 128, NTOK + 2 * PADT], BF16)
    VV = dscr("VV", [NTOK + 2 * PADT, AW], BF16)
    UT = dscr("UT", [18, 128, NTOK], BF16)
    DS = dscr("DS", [3, 6, NTOK], F32)
    ZZ = dscr("ZZ", [NSLOT, SLOT, 2 * D], BF16)
    YY = dscr("YY", [NSLOT, SLOT, D], BF16)

    with ExitStack() as _stk:
        pb0 = _stk.enter_context(nc.psum_tensor("pb0", [128, 512], F32))
        pb1 = _stk.enter_context(nc.psum_tensor("pb1", [128, 512], F32))
        pb2 = _stk.enter_context(nc.psum_tensor("pb2", [128, 512], F32))
        pb3 = _stk.enter_context(nc.psum_tensor("pb3", [128, 512], F32))
        pb4 = _stk.enter_context(nc.psum_tensor("pb4", [128, 512], F32))
        pb5 = _stk.enter_context(nc.psum_tensor("pb5", [128, 512], F32))
        pb6 = _stk.enter_context(nc.psum_tensor("pb6", [128, 512], F32))
        pb7 = _stk.enter_context(nc.psum_tensor("pb7", [128, 512], F32))
        ones_bf = _stk.enter_context(sbt("ones", [128, 128], BF16))
        ident = _stk.enter_context(sbt("ident", [128, 128], F32))
        MODT = _stk.enter_context(sbt("MODT", [128, NL, 96, NSLOT], F32))
        GS1 = _stk.enter_context(sbt("GS1", [128, NL, KC, NSLOT], F32))
        GS2 = _stk.enter_context(sbt("GS2", [128, NL, KC, NSLOT], F32))
        GB = _stk.enter_context(sbt("GB", [128, NL, KC, NSLOT], F32))
        G2B2 = _stk.enter_context(sbt("G2B2", [128, NL, KC, NSLOT], F32))
        b1s = _stk.enter_context(sbt("b1s", [128, NL, 64], F32))
        fgs = _stk.enter_context(sbt("fgs", [128, KC], F32))
        pens = _stk.enter_context(sbt("pens", [128, 9], F32))
        g.banks = [pb0, pb1, pb2, pb3, pb4, pb5, pb6, pb7]

        ev_c = []
        ev_c.append(g.op("dve", lambda e: e.memset(ones_bf[:], 1.0)))
        ev_c.append(g.dma("sp", ident[:], ident_in, "const"))
        ev_c.append(g.dma("sp", b1s[:], b1T, "const"))
        ev_c.append(g.dma("sp", fgs[:], fgT, "const"))
        ev_c.append(g.dma("sp", pens[:], pen_in, "const"))

        cast_ev = {}

        def cast(name, dst, src, rows, lo=0):
            evs = []
            r = lo
            while r < lo + rows:
                n = min(128, lo + rows - r)
                evs.append(g.dma("pool", dst[r:r + n, :], src[r:r + n, :], "cast_" + name))
                r += n
            cast_ev[name] = evs[-1]

        def cast_layer(l):
            if l % 2 == 0:
                la = l // 2
                cast("qkv%d" % la, WQKVb, w_qkv, D, la * D)
                cast("wo%d" % la, WOb, w_o, AW, la * AW)
            else:
                lf = l // 2
                cast("wf%d" % lf, WFb, w_f, D, lf * D)
            cast("w1_%d" % l, W1b, w1, D, l * D)
            cast("w2_%d" % l, W2b, w2, DFF, l * DFF)

        with sbt("zpad", [128, 4608], BF16) as zpad:
            ez = g.op("dve", lambda e: e.memset(zpad[:], 0.0))
            evz = []
            for side in (0, 1):
                t0 = 0 if side == 0 else PADT + NTOK
                for hh in range(0, 18, 3):
                    evz.append(g.dma("sp", KT[hh:hh + 3, :, t0:t0 + PADT].rearrange("h e t -> e h t"),
                                     zpad[:, 0:3 * PADT].rearrange("p (h t) -> p h t", h=3), "zpad", [ez]))
                for rr in range(0, PADT, 128):
                    evz.append(g.dma("sp", VV[t0 + rr:t0 + rr + 128, :], zpad[:, 0:AW], "zpad", [ez]))
            g.barrier()
        pad_ev = evz[-1]

        for l in range(min(n_layers, NL)):
            cast_layer(l)

        with ExitStack() as _stk:
            p0x0 = _stk.enter_context(sbt("p0x0", [128, 4, D], F32))
            p0x1 = _stk.enter_context(sbt("p0x1", [128, 4, D], F32))
            p0t0 = _stk.enter_context(sbt("p0t0", [128, KC, 512], F32))
            p0t1 = _stk.enter_context(sbt("p0t1", [128, KC, 512], F32))
            xin = Rot([p0x0, p0x1])
            xtr = Rot([p0t0, p0t1])
            xv = x_in.rearrange("(t s p) f -> t p s f", s=4, p=128)
            p0_out = []
            for t in range(NTOK // 512):
                ii, xi, xfree = xin.get()
                ld = g.dma("sp", xi[:], xv[t], "L0_%d" % ii, [xfree])
                oi, xo, ofree = xtr.get()
                last_pe = None
                evs_ev = []
                for kc in range(KC):
                    b, ps, bfree = g.bank()
                    for s in range(4):
                        last_pe = g.op(
                            "pe", lambda e, ps=ps, xi=xi, s=s, kc=kc: e.transpose(
                                ps[:, s * 128:(s + 1) * 128], xi[:, s, kc * 128:(kc + 1) * 128], ident[:]),
                            [ld, bfree, ev_c] if s == 0 else (), sig=(s == 3))
                    eng = "act" if kc % 2 == 0 else "dve"
                    if eng == "act":
                        ev = g.op("act", lambda e, xo=xo, ps=ps, kc=kc: e.copy(xo[:, kc, :], ps[:]), [last_pe, ofree])
                    else:
                        ev = g.op("dve", lambda e, xo=xo, ps=ps, kc=kc: e.tensor_copy(xo[:, kc, :], ps[:]), [last_pe, ofree])
                    g.bank_release(b, ev)
                    evs_ev.append(ev)
                xin.release(ii, last_pe)
                st = g.dma("sp", XTv[:, :, t * 512:(t + 1) * 512], xo[:], "S0_%d" % oi, evs_ev)
                xtr.release(oi, st)
                p0_out.append(st)
        g.barrier()

        with ExitStack() as _stk:
            cact = _stk.enter_context(sbt("cact", [128, KC, NSLOT], F32))
            paw0 = _stk.enter_context(sbt("paw0", [128, KC, 512], F32))
            paw1 = _stk.enter_context(sbt("paw1", [128, KC, 512], F32))
            badas = _stk.enter_context(sbt("badas", [128, NL, 96], F32))
            n1s = _stk.enter_context(sbt("n1s", [128, NL, KC], F32))
            n2s = _stk.enter_context(sbt("n2s", [128, NL, KC], F32))
            bfs = _stk.enter_context(sbt("bfs", [128, 2, KC], F32))
            b2s = _stk.enter_context(sbt("b2s", [128, NL, KC], F32))
            e0 = g.dma("sp", cact[:], cT_in, "L2_0")
            e1 = [g.dma("sp", badas[:], b_adaT, "L2_1"), g.dma("sp", n1s[:], n1T, "L2_1"),
                  g.dma("sp", n2s[:], n2T, "L2_1"), g.dma("sp", bfs[:], bfT, "L2_1"), g.dma("sp", b2s[:], b2T, "L2_1")]
            esil = g.op("act", lambda e: e.activation(cact[:], cact[:], AF.Silu), [e0])
            wrot = Rot([paw0, paw1])
            wav = w_ada.rearrange("(l kc p) c -> l p kc c", l=NL, p=128)
            mod_ev = []
            for l in range(NL):
                for blk in range(24):
                    wi, wt, wfree = wrot.get()
                    ld = g.dma("sp", wt[:], wav[l][:, :, blk * 512:(blk + 1) * 512], "L0_%d" % wi, [wfree])
                    b, ps, bfree = g.bank()
                    lastmm = None
                    for mm in range(4):
                        for kc in range(KC):
                            lastmm = g.op(
                                "pe", lambda e, ps=ps, wt=wt, mm=mm, kc=kc: e.matmul(
                                    ps[:, mm * 4:(mm + 1) * 4], wt[:, kc, mm * 128:(mm + 1) * 128], cact[:, kc, :],
                                    start=(kc == 0), stop=(kc == KC - 1)),
                                [ld, esil, bfree] if (mm == 0 and kc == 0) else (), sig=(mm == 3 and kc == KC - 1))
                    wrot.release(wi, lastmm)
                    ev = g.op("dve", lambda e, ps=ps, l=l, blk=blk: e.tensor_copy(
                        MODT[:, l, blk * 4:(blk + 1) * 4, :], ps[:, 0:16].rearrange("p (m s) -> p m s", m=4)), [lastmm])
                    g.bank_release(b, ev)
                    mod_ev.append(ev)
            fin = []
            for l in range(NL):
                for s in range(NSLOT):
                    ev = g.op("dve", lambda e, l=l, s=s: e.tensor_tensor(
                        MODT[:, l, :, s], MODT[:, l, :, s], badas[:, l, :], ALU.add), [mod_ev, e1])
                    ev = g.op("dve", lambda e, l=l, s=s: e.scalar_tensor_tensor(
                        GS1[:, l, :, s], MODT[:, l, 16:32, s], 1.0, n1s[:, l, :], ALU.add, ALU.mult), [ev])
                    ev2 = g.op("dve", lambda e, l=l, s=s: e.scalar_tensor_tensor(
                        GS2[:, l, :, s], MODT[:, l, 64:80, s], 1.0, n2s[:, l, :], ALU.add, ALU.mult), [ev])
                    ev3 = g.op("dve", lambda e, l=l, s=s: e.tensor_tensor(
                        G2B2[:, l, :, s], MODT[:, l, 80:96, s], b2s[:, l, :], ALU.mult), [ev2])
                    if l % 2 == 1:
                        ev3 = g.op("dve", lambda e, l=l, s=s: e.tensor_tensor(
                            GB[:, l, :, s], MODT[:, l, 32:48, s], bfs[:, l // 2, :], ALU.mult), [ev3])
                    fin.append(ev3)
        g.barrier()

        def col(t, l, kc, s):
            return t[:, l, kc, s:s + 1]

        def modc(l, which, kc, s):
            return MODT[:, l, which * 16 + kc, s:s + 1]

        def norm_stats(xt, sqrot, x_ev, rstd, rstd_free):
            b, ps, bfree = g.bank()
            last = None
            for kc in range(KC):
                si, sq, sfree = sqrot.get()
                xe = x_ev[kc] if (isinstance(x_ev, list) and len(x_ev) == KC) else x_ev
                es = g.op("act", lambda e, sq=sq, kc=kc: e.activation(sq[:], xt[:, kc, :], AF.Square), [xe, sfree])
                last = g.op("pe", lambda e, ps=ps, sq=sq, kc=kc: e.matmul(
                    ps[:], ones_bf[:], sq[:], start=(kc == 0), stop=(kc == KC - 1)),
                    [es, bfree, ev_c] if kc == 0 else [es], sig=True)
                sqrot.release(si, last)
            e1_ = g.op("act", lambda e: e.activation(rstd[:], ps[:], AF.Sqrt, bias=EPS, scale=1.0 / D), [last, rstd_free])
            e2_ = g.op("dve", lambda e: e.reciprocal(rstd[:], rstd[:]), [e1_])
            g.bank_release(b, e1_)
            return e2_

        def ada_apply(xt, h, rstd, rstd_ev, tmprot, x_ev, h_free, gs_t, sh_fn, l, slot_of_col, remap=False):
            evs = []
            for kc in range(KC):
                ti, tmp, tfree = tmprot.get()
                xe = x_ev[kc] if (isinstance(x_ev, list) and len(x_ev) == KC) else x_ev
                e1_ = g.op("dve", lambda e, tmp=tmp, kc=kc: e.tensor_tensor(tmp[:], xt[:, kc, :], rstd[:], ALU.mult),
                           [xe, rstd_ev, tfree])
                ev = None
                for (c0, c1, s) in slot_of_col:
                    if remap:
                        o_ap = h[:, kc, :].rearrange("p (q a b) -> p q a b", q=4, a=4)[:, :, s, :]
                        i_ap = tmp[:, c0:c1].rearrange("p (q b) -> p q b", q=4)
                    else:
                        o_ap = h[:, kc, c0:c1]
                        i_ap = tmp[:, c0:c1]
                    ev = g.op("act", lambda e, o_ap=o_ap, i_ap=i_ap, kc=kc, s=s: e.activation(
                        o_ap, i_ap, AF.Identity, bias=sh_fn(kc, s), scale=col(gs_t, l, kc, s)),
                        [e1_, h_free])
                tmprot.release(ti, ev)
                evs.append(ev)
            return evs

        def phase_A1(l):
            la = l // 2
            with ExitStack() as _stk:
                a1x0 = _stk.enter_context(sbt("a1x0", [128, KC, 512], F32))
                a1x1 = _stk.enter_context(sbt("a1x1", [128, KC, 512], F32))
                H = _stk.enter_context(sbt("a1h", [128, KC, 1024], BF16))
                a1w0 = _stk.enter_context(sbt("a1w0", [128, KC, 768], BF16))
                a1w1 = _stk.enter_context(sbt("a1w1", [128, KC, 768], BF16))
                a1q0 = _stk.enter_context(sbt("a1q0", [128, 6, 512], BF16))
                a1q1 = _stk.enter_context(sbt("a1q1", [128, 6, 512], BF16))
                a1v0 = _stk.enter_context(sbt("a1v0", [128, 768], BF16))
                a1v1 = _stk.enter_context(sbt("a1v1", [128, 768], BF16))
                sq0 = _stk.enter_context(sbt("a1sq0", [128, 512], BF16))
                sq1 = _stk.enter_context(sbt("a1sq1", [128, 512], BF16))
                tm0 = _stk.enter_context(sbt("a1tm0", [128, 512], F32))
                tm1 = _stk.enter_context(sbt("a1tm1", [128, 512], F32))
                rstd = _stk.enter_context(sbt("a1rs", [128, 512], F32))
                xrot = Rot([a1x0, a1x1]); wrot = Rot([a1w0, a1w1]); qrot = Rot([a1q0, a1q1]); vrot = Rot([a1v0, a1v1])
                sqrot = Rot([sq0, sq1]); tmrot = Rot([tm0, tm1])
                wv = WQKVb.rearrange("(l kc p) c -> l p kc c", l=2, p=128)[la]
                rstd_free = None
                h_free = None
                for hb in range(NTOK // 1024):
                    slot = hb // 2
                    h_evs = []
                    for tq in range(2):
                        t0 = hb * 1024 + tq * 512
                        xi, xt, xfree = xrot.get()
                        ld = g.dma("sp", xt[:], XTv[:, :, t0:t0 + 512], "L0_%d" % xi, [xfree])
                        rs_ev = norm_stats(xt, sqrot, ld, rstd, rstd_free)
                        hv = H[:, :, tq * 512:(tq + 1) * 512]
                        evs = ada_apply(xt, hv, rstd, rs_ev, tmrot, ld, h_free, GS1,
                                        lambda kc, s: modc(l, 0, kc, s), l, [(0, 512, slot)])
                        xrot.release(xi, evs[-1])
                        rstd_free = evs[-1]
                        h_evs.append(evs)
                    h_read = []
                    for blk in range(9):
                        gi, which = blk // 3, blk % 3
                        wi, wt, wfree = wrot.get()
                        c0 = gi * 2304 + which * 768
                        ld = g.dma("sp", wt[:], wv[:, :, c0:c0 + 768], "L1_%d" % wi, [wfree, cast_ev["qkv%d" % la]])
                        lastmm = None
                        if which < 2:
                            dst = QT if which == 0 else KT
                            toff = 0 if which == 0 else PADT
                            for tq in range(2):
                                t0 = hb * 1024 + tq * 512
                                qi, qs, qfree = qrot.get()
                                evq = []
                                for hh in range(6):
                                    b, ps, bfree = g.bank()
                                    for kc in range(KC):
                                        lastmm = g.op("pe", lambda e, ps=ps, wt=wt, hh=hh, kc=kc, tq=tq: e.matmul(
                                            ps[:], wt[:, kc, hh * 128:(hh + 1) * 128], H[:, kc, tq * 512:(tq + 1) * 512],
                                            start=(kc == 0), stop=(kc == KC - 1)),
                                            [ld, bfree, h_evs[tq]] if kc == 0 else (), sig=(kc == KC - 1))
                                    eng = "act" if hh % 2 == 0 else "dve"
                                    if eng == "act":
                                        ev = g.op("act", lambda e, qs=qs, ps=ps, hh=hh: e.copy(qs[:, hh, :], ps[:]), [lastmm, qfree])
                                    else:
                                        ev = g.op("dve", lambda e, qs=qs, ps=ps, hh=hh: e.tensor_copy(qs[:, hh, :], ps[:]), [lastmm, qfree])
                                    g.bank_release(b, ev)
                                    evq.append(ev)
                                st = g.dma("sp", dst[gi * 6:(gi + 1) * 6, :, toff + t0:toff + t0 + 512].rearrange("h e t -> e h t"),
                                           qs[:], "S0_%d" % qi, evq)
                                qrot.release(qi, st)
                        else:
                            for sub in range(8):
                                t0 = hb * 1024 + sub * 128
                                vi, vs, vfree = vrot.get()
                                evv = []
                                for (cc0, cc1) in ((0, 512), (512, 768)):
                                    b, ps, bfree = g.bank()
                                    for kc in range(KC):
                                        lastmm = g.op("pe", lambda e, ps=ps, wt=wt, kc=kc, sub=sub, cc0=cc0, cc1=cc1: e.matmul(
                                            ps[:, 0:cc1 - cc0], H[:, kc, sub * 128:(sub + 1) * 128], wt[:, kc, cc0:cc1],
                                            start=(kc == 0), stop=(kc == KC - 1)),
                                            [ld, bfree, h_evs] if kc == 0 else (), sig=(kc == KC - 1))
                                    if cc0 == 0:
                                        ev = g.op("act", lambda e, vs=vs, ps=ps: e.copy(vs[:, 0:512], ps[:, 0:512]), [lastmm, vfree])
                                    else:
                                        ev = g.op("dve", lambda e, vs=vs, ps=ps: e.tensor_copy(vs[:, 512:768], ps[:, 0:256]), [lastmm, vfree])
                                    g.bank_release(b, ev)
                                    evv.append(ev)
                                st = g.dma("sp", VV[PADT + t0:PADT + t0 + 128, gi * 768:(gi + 1) * 768], vs[:], "S1_%d" % vi, evv)
                                vrot.release(vi, st)
                        wrot.release(wi, lastmm)
                        h_read.append(lastmm)
                    h_free = h_read
            g.barrier()

        def phase_A2(l):
            with ExitStack() as _stk:
                abias = _stk.enter_context(sbt("abias", [128, 3, 2, 384], F32))
                a2q0 = _stk.enter_context(sbt("a2q0", [128, 6, 1024], BF16))
                a2q1 = _stk.enter_context(sbt("a2q1", [128, 6, 1024], BF16))
                a2k0 = _stk.enter_context(sbt("a2k0", [128, 6, 3072], BF16))
                a2k1 = _stk.enter_context(sbt("a2k1", [128, 6, 3072], BF16))
                a2u0 = _stk.enter_context(sbt("a2u0", [128, 6, 1024], BF16))
                a2u1 = _stk.enter_context(sbt("a2u1", [128, 6, 1024], BF16))
                a2d0 = _stk.enter_context(sbt("a2d0", [1, 6, 1024], F32))
                a2v0 = _stk.enter_context(sbt("a2v0", [128, 2, 768], BF16))
                a2v1 = _stk.enter_context(sbt("a2v1", [128, 2, 768], BF16))
                a2v2 = _stk.enter_context(sbt("a2v2", [128, 2, 768], BF16))
                a2t0 = _stk.enter_context(sbt("a2t0", [128, 2, 384], F32))
                a2t1 = _stk.enter_context(sbt("a2t1", [128, 2, 384], F32))
                a2p0 = _stk.enter_context(sbt("a2p0", [128, 2, 384], BF16))
                a2p1 = _stk.enter_context(sbt("a2p1", [128, 2, 384], BF16))
                a2p2 = _stk.enter_context(sbt("a2p2", [128, 2, 384], BF16))
                a2t2 = _stk.enter_context(sbt("a2t2", [128, 2, 384], F32))
                a2v3 = _stk.enter_context(sbt("a2v3", [128, 2, 768], BF16))
                eab = g.dma("sp", abias[:], abias_in.rearrange("g t p c -> p g t c"), "L2_0")
                qrot = Rot([a2q0, a2q1]); krot = Rot([a2k0, a2k1]); urot = Rot([a2u0, a2u1]); drot = Rot([a2d0])
                vrot = Rot([a2v0, a2v1, a2v2, a2v3]); trot = Rot([a2t0, a2t1, a2t2]); prot = Rot([a2p0, a2p1, a2p2])
                scale = 128.0 ** -0.5
                for mt in range(NTOK // 1024):
                    P0 = mt * 1024
                    for gi in range(3):
                        d = DIL[gi]
                        blk = 64 * d
                        halo = blk
                        qi, qs, qfree = qrot.get()
                        ki, ks, kfree = krot.get()
                        ldq = g.dma("sp", qs[:], QT[gi * 6:(gi + 1) * 6, :, P0:P0 + 1024].rearrange("h e t -> e h t"), "L0_%d" % qi, [qfree])
                        kw = 1024 + 2 * halo
                        ldk = g.dma("sp", ks[:, :, 0:kw],
                                    KT[gi * 6:(gi + 1) * 6, :, PADT + P0 - halo:PADT + P0 + 1024 + halo].rearrange("h e t -> e h t"),
                                    "L1_%d" % ki, [kfree, pad_ev])
                        ui, us, ufree = urot.get()
                        di, dst_, dfree = drot.get()
                        last_reads = []
                        outs_ev = []
                        pending = None

                        def part2(stt):
                            (vi, vt, ldv, pi, pt, xA, xB, q0) = stt
                            bU, psU, fU = g.bank()
                            bD, psD, fD = g.bank()
                            for hh in range(6):
                                g.op("pe", lambda e, psU=psU, vt=vt, pt=pt, hh=hh: e.matmul(
                                    psU[:, hh * 64:(hh + 1) * 64], vt[:, 0, hh * 128:(hh + 1) * 128], pt[:, 0, hh * 64:(hh + 1) * 64],
                                    start=True, stop=False), [xA, xB, ldv, fU, fD] if hh == 0 else (), sig=False)
                                g.op("pe", lambda e, psU=psU, vt=vt, pt=pt, hh=hh: e.matmul(
                                    psU[:, hh * 64:(hh + 1) * 64], vt[0:64, 1, hh * 128:(hh + 1) * 128], pt[0:64, 1, hh * 64:(hh + 1) * 64],
                                    start=False, stop=True), (), sig=False)
                            g.op("pe", lambda e, psD=psD, pt=pt: e.matmul(
                                psD[:, 0:384], ones_bf[:, :], pt[:, 0, :], start=True, stop=False), (), sig=False)
                            lastU = g.op("pe", lambda e, psD=psD, pt=pt: e.matmul(
                                psD[:, 0:384], ones_bf[0:64, :], pt[0:64, 1, :], start=False, stop=True), (), sig=True)
                            vrot.release(vi, lastU)
                            prot.release(pi, lastU)
                            eU = g.op("act", lambda e, psU=psU, q0=q0: e.copy(
                                us[:, :, sl(q0, 64, d)], psU[:, 0:384].rearrange("p (h q) -> p h q", h=6)), [lastU, ufree])
                            eD = g.op("dve", lambda e, psD=psD, q0=q0: e.tensor_copy(
                                dst_[0:1, :, sl(q0, 64, d)], psD[0:1, 0:384].rearrange("p (h q) -> p h q", h=6)), [lastU, dfree])
                            g.bank_release(bU, eU)
                            g.bank_release(bD, eD)
                            outs_ev.extend([eU, eD])

                        for bi in range(1024 // blk):
                            for r in range(d):
                                q0 = bi * blk + r
                                pos0 = P0 + bi * blk
                                colA = 0
                                colB = 0
                                if pos0 % SLOT == 0:
                                    colA = 1 + pos0 // SLOT
                                if (pos0 + blk) % SLOT == 0:
                                    colB = 5 + pos0 // SLOT
                                vi, vt, vfree = vrot.get()
                                row0 = PADT + pos0 - blk + r
                                vsrc = VV[:, gi * 768:(gi + 1) * 768]
                                ldv = [g.dma("sp", vt[:, 0, :], vsrc[sl(row0, 128, d), :], "L3_%d" % vi, [vfree, pad_ev]),
                                       g.dma("sp", vt[0:64, 1, :], vsrc[sl(row0 + 128 * d, 64, d), :], "L3_%d" % vi)]
                                bA, psA, fA = g.bank()
                                bB, psB, fB = g.bank()
                                lastS = None
                                for hh in range(6):
                                    kA = ks[:, hh, sl(q0, 128, d)]
                                    kB = ks[:, hh, sl(q0 + 128 * d, 64, d)]
                                    qq = qs[:, hh, sl(q0, 64, d)]
                                    g.op("pe", lambda e, psA=psA, kA=kA, qq=qq, hh=hh: e.matmul(
                                        psA[:, hh * 64:(hh + 1) * 64], kA, qq, start=True, stop=True),
                                        [ldq, ldk, fA, fB] if hh == 0 else (), sig=False)
                                    lastS = g.op("pe", lambda e, psB=psB, kB=kB, qq=qq, hh=hh: e.matmul(
                                        psB[0:64, hh * 64:(hh + 1) * 64], kB, qq, start=True, stop=True), (), sig=(hh == 5))
                                ti, tt, tfree = trot.get()
                                eA = g.op("dve", lambda e, tt=tt, psA=psA, gi=gi: e.scalar_tensor_tensor(
                                    tt[:, 0, :], psA[:, 0:384], scale, abias[:, gi, 0, :], ALU.mult, ALU.add), [lastS, tfree, eab])
                                eB = g.op("dve", lambda e, tt=tt, psB=psB, gi=gi: e.scalar_tensor_tensor(
                                    tt[0:64, 1, :], psB[0:64, 0:384], scale, abias[0:64, gi, 1, :], ALU.mult, ALU.add), [lastS])
                                g.bank_release(bA, eA)
                                g.bank_release(bB, eB)
                                pi, pt, pfree = prot.get()
                                xA = g.op("act", lambda e, pt=pt, tt=tt, colA=colA: e.activation(
                                    pt[:, 0, :], tt[:, 0, :], AF.Exp, bias=pens[:, colA:colA + 1], scale=1.0), [eA, pfree, ev_c])
                                xB = g.op("act", lambda e, pt=pt, tt=tt, colB=colB: e.activation(
                                    pt[0:64, 1, :], tt[0:64, 1, :], AF.Exp, bias=pens[0:64, colB:colB + 1], scale=1.0), [eB])
                                trot.release(ti, xB)
                                last_reads.append(lastS)
                                if pending is not None:
                                    part2(pending)
                                pending = (vi, vt, ldv, pi, pt, xA, xB, q0)
                        part2(pending)
                        qrot.release(qi, last_reads[-1])
                        krot.release(ki, last_reads[-1])
                        stu = g.dma("sp", UT[gi * 6:(gi + 1) * 6, :, P0:P0 + 1024].rearrange("h e t -> e h t"), us[:], "S0_%d" % ui, outs_ev)
                        std = g.dma("sp", DS[gi:gi + 1, :, P0:P0 + 1024], dst_[:], "S1_%d" % di, outs_ev)
                        urot.release(ui, stu)
                        drot.release(di, std)
            g.barrier()

        def phase_tail(l, kind, is_last):
            la = l // 2
            KCM = 18 if kind == "attn" else 16
            with ExitStack() as _stk:
                Y = _stk.enter_context(sbt("tY", [128, KC, 512], F32))
                Mx = _stk.enter_context(sbt("tM", [128, KCM, 512], BF16))
                H2 = _stk.enter_context(sbt("tH", [128, KC, 512], BF16))
                wp0 = _stk.enter_context(sbt("tWp0", [128, KCM, 256], BF16))
                wp1 = _stk.enter_context(sbt("tWp1", [128, KCM, 256], BF16))
                w1a = _stk.enter_context(sbt("tW1a", [128, KC, 512], BF16))
                w1b_ = _stk.enter_context(sbt("tW1b", [128, KC, 512], BF16))
                w2a = _stk.enter_context(sbt("tW2a", [128, 4, D], BF16))
                w2b_ = _stk.enter_context(sbt("tW2b", [128, 4, D], BF16))
                u0 = _stk.enter_context(sbt("tU0", [128, 4, 512], BF16))
                u1 = _stk.enter_context(sbt("tU1", [128, 4, 512], BF16))
                r0 = _stk.enter_context(sbt("tR0", [128, 512], F32))
                r1 = _stk.enter_context(sbt("tR1", [128, 512], F32))
                sq0 = _stk.enter_context(sbt("tsq0", [128, 512], BF16))
                sq1 = _stk.enter_context(sbt("tsq1", [128, 512], BF16))
                tm0 = _stk.enter_context(sbt("ttm0", [128, 512], F32))
                tm1 = _stk.enter_context(sbt("ttm1", [128, 512], F32))
                rstd = _stk.enter_context(sbt("trs", [128, 512], F32))
                dsl = rcp = selh = gp0 = gp1 = gs0 = gs1 = fgm = ot0 = ot1 = None
                if kind == "attn":
                    dsl = _stk.enter_context(sbt("tds", [6, 3, 512], F32))
                    rcp = _stk.enter_context(sbt("trc", [6, 512], F32))
                    selh = _stk.enter_context(sbt("tsel", [6, 6, 128], F32))
                else:
                    gp0 = _stk.enter_context(sbt("tg0", [128, D], BF16))
                    gp1 = _stk.enter_context(sbt("tg1", [128, D], BF16))
                    gs0 = _stk.enter_context(sbt("tg2", [128, D], BF16))
                    gs1 = _stk.enter_context(sbt("tg3", [128, D], BF16))
                    fgm = _stk.enter_context(sbt("tfg", [128, 2, 128], BF16))
                if is_last:
                    ot0 = _stk.enter_context(sbt("tO0", [128, 1024], F32))
                    ot1 = _stk.enter_context(sbt("tO1", [128, 1024], F32))
                if kind == "attn":
                    esel = [g.dma("sp", selh[:], selh_in, "L2_0")]
                else:
                    esel = [g.dma("sp", fgm[:], fg_in.rearrange("a p n -> p a n"), "L2_0")]
                wprot = Rot([wp0, wp1]); w1rot = Rot([w1a, w1b_]); w2rot = Rot([w2a, w2b_]); urot = Rot([u0, u1])
                rrot = Rot([r0, r1]); sqrot = Rot([sq0, sq1]); tmrot = Rot([tm0, tm1]); orot = Rot([ot0, ot1]) if is_last else None
                gprot = Rot([gp0, gp1]); gsrot = Rot([gs0, gs1])
                if kind == "attn":
                    wpv = WOb.rearrange("(l kc p) c -> l p kc c", l=2, p=128)[la]
                    wp_cast = cast_ev["wo%d" % la]
                else:
                    wpv = WFb.rearrange("(l kc p) c -> l p kc c", l=2, p=128)[la]
                    wp_cast = cast_ev["wf%d" % la]
                w1v = W1b.rearrange("(l kc p) c -> l p kc c", l=NL, p=128)[l]
                w2v = W2b.rearrange("(l j fc p) c -> l j p fc c", l=NL, j=16, p=128)[l]
                NT = NTOK // 512
                st_ = {"m_free": None, "ds_free": None, "rc_free": None}
                ych_free = [None] * KC

                def produce_mx(t):
                    t0 = t * 512
                    slot = t0 // SLOT
                    m_evs = []
                    if kind == "attn":
                        ldm = g.dma("sp", Mx[:], UT[:, :, t0:t0 + 512].rearrange("h e t -> e h t"), "L4_1", [st_["m_free"]])
                        ldd = g.dma("sp", dsl[:], DS[:, :, t0:t0 + 512].rearrange("g h t -> h g t"), "L4_2", [st_["ds_free"]])
                        ea = g.op("dve", lambda e: e.tensor_tensor(rcp[:], dsl[:, 0, :], dsl[:, 1, :], ALU.add), [ldd, st_["rc_free"]])
                        ea = g.op("dve", lambda e: e.tensor_tensor(rcp[:], rcp[:], dsl[:, 2, :], ALU.add), [ea])
                        st_["ds_free"] = ea
                        ea = g.op("dve", lambda e: e.reciprocal(rcp[:], rcp[:]), [ea])
                        lastb = None
                        for hh in range(6):
                            b, ps, bfree = g.bank()
                            lastb = g.op("pe", lambda e, ps=ps, hh=hh: e.matmul(ps[:], selh[:, hh, :], rcp[:], start=True, stop=True),
                                         [ea, esel, bfree])
                            ev = None
                            for gi in range(3):
                                hd = gi * 6 + hh
                                ev = g.op("dve", lambda e, ps=ps, hd=hd: e.tensor_tensor(Mx[:, hd, :], Mx[:, hd, :], ps[:], ALU.mult),
                                          [lastb, ldm])
                                m_evs.append(ev)
                            g.bank_release(b, ev)
                        st_["rc_free"] = lastb
                    else:
                        c_ = slot
                        tq = (t0 % SLOT) // 512
                        for s in range(4):
                            pi, pt, pfree = gprot.get()
                            si, stl, sfree = gsrot.get()
                            lds = g.dma("sp", stl[:], YY[c_, tq * 512 + s * 128:tq * 512 + (s + 1) * 128, :], "L1_%d" % si, [sfree])
                            ldp = None
                            r0_ = 512 * c_ + 128 * tq + 32 * s
                            for k1 in range(4):
                                ldp = g.dma("sp", pt[32 * k1:32 * (k1 + 1), :], YY[k1, r0_:r0_ + 32, :], "L0_%d" % pi, [pfree] if k1 == 0 else ())
                            last = None
                            for q in range(4):
                                b, ps, bfree = g.bank()
                                for jj in range(4):
                                    kc = q * 4 + jj
                                    g.op("pe", lambda e, ps=ps, pt=pt, kc=kc, jj=jj: e.matmul(
                                        ps[:, jj * 128:(jj + 1) * 128], pt[:, kc * 128:(kc + 1) * 128], fgm[:, 0, :], start=True, stop=False),
                                        [ldp, lds, esel, bfree] if jj == 0 else (), sig=False)
                                    last = g.op("pe", lambda e, ps=ps, stl=stl, kc=kc, jj=jj: e.matmul(
                                        ps[:, jj * 128:(jj + 1) * 128], stl[:, kc * 128:(kc + 1) * 128], fgm[:, 1, :], start=False, stop=True),
                                        (), sig=(jj == 3))
                                dstv = Mx[:, q * 4:(q + 1) * 4, s * 128:(s + 1) * 128]
                                srcv = ps[:].rearrange("p (j n) -> p j n", j=4)
                                if q % 2 == 0:
                                    ev = g.op("act", lambda e, dstv=dstv, srcv=srcv: e.copy(dstv, srcv), [last, st_["m_free"]])
                                else:
                                    ev = g.op("dve", lambda e, dstv=dstv, srcv=srcv: e.tensor_copy(dstv, srcv), [last, st_["m_free"]])
                                g.bank_release(b, ev)
                                m_evs.append(ev)
                            gprot.release(pi, last)
                            gsrot.release(si, last)
                    return m_evs

                wp_q = []; w12_q = []

                def issue_wp(n):
                    if n >= NT * 8:
                        return
                    mb = n % 8
                    wi, wt, wfree = wprot.get()
                    ldw = g.dma("sp", wt[:], wpv[:, :, mb * 256:(mb + 1) * 256], "L3_%d" % wi, [wfree, wp_cast])
                    wp_q.append((wi, wt, ldw))

                def issue_w12(n):
                    if n >= NT * 16:
                        return
                    j = n % 16
                    i1, w1t, f1 = w1rot.get()
                    i2, w2t, f2 = w2rot.get()
                    ld1 = g.dma("sp", w1t[:], w1v[:, :, j * 512:(j + 1) * 512], "L5_%d" % i1, [f1, cast_ev["w1_%d" % l]])
                    ld2 = g.dma("sp", w2t[:], w2v[j], "L6_%d" % i2, [f2, cast_ev["w2_%d" % l]])
                    w12_q.append((i1, w1t, ld1, i2, w2t, ld2))

                issue_wp(0); issue_wp(1)
                m_evs_next = produce_mx(0)
                issue_w12(0); issue_w12(1)
                h_free = None; rstd_free = None
                for t in range(NT):
                    t0 = t * 512
                    slot = t0 // SLOT
                    ldx = [g.dma("sp", Y[:, m, :], XTv[:, m, t0:t0 + 512], "Y%d" % m, [ych_free[m]]) for m in range(KC)]
                    m_evs = m_evs_next
                    y_evs = [None] * KC
                    lastp = None
                    for mb in range(8):
                        wi, wt, ldw = wp_q.pop(0)
                        for m2 in range(2):
                            m = mb * 2 + m2
                            b, ps, bfree = g.bank()
                            for kc in range(KCM):
                                lastp = g.op("pe", lambda e, ps=ps, wt=wt, kc=kc, m2=m2: e.matmul(
                                    ps[:], wt[:, kc, m2 * 128:(m2 + 1) * 128], Mx[:, kc, :], start=(kc == 0), stop=(kc == KCM - 1)),
                                    [ldw, bfree, m_evs] if kc == 0 else (), sig=(kc == KCM - 1))
                            ev = g.op("dve", lambda e, ps=ps, m=m: e.scalar_tensor_tensor(
                                Y[:, m, :], ps[:], modc(l, 2, m, slot), Y[:, m, :], ALU.mult, ALU.add), [lastp, ldx[m]])
                            g.bank_release(b, ev)
                            if kind == "fft":
                                ev = g.op("dve", lambda e, m=m: e.tensor_scalar(
                                    Y[:, m, :], Y[:, m, :], col(GB, l, m, slot), None, ALU.add), [ev])
                            y_evs[m] = ev
                        wprot.release(wi, lastp)
                        issue_wp(t * 8 + mb + 2)
                    st_["m_free"] = lastp
                    rs_ev = norm_stats(Y, sqrot, y_evs, rstd, rstd_free)
                    h_evs = ada_apply(Y, H2, rstd, rs_ev, tmrot, y_evs, h_free, GS2,
                                      lambda kc, s: modc(l, 3, kc, s), l, [(0, 512, slot)])
                    rstd_free = h_evs[-1]
                    if t + 1 < NT:
                        m_evs_next = produce_mx(t + 1)
                    yb = []
                    for m in range(KC):
                        yb.append(g.op("dve", lambda e, m=m: e.tensor_scalar(
                            Y[:, m, :], Y[:, m, :], col(G2B2, l, m, slot), None, ALU.add), [h_evs[m]]))
                    lastmm = None
                    for j in range(16):
                        i1, w1t, ld1, i2, w2t, ld2 = w12_q.pop(0)
                        ui, ut, ufree = urot.get()
                        u_evs = []
                        last1 = None
                        for fc in range(4):
                            b, ps, bfree = g.bank()
                            for kc in range(KC):
                                last1 = g.op("pe", lambda e, ps=ps, w1t=w1t, kc=kc, fc=fc: e.matmul(
                                    ps[:], w1t[:, kc, fc * 128:(fc + 1) * 128], H2[:, kc, :], start=(kc == 0), stop=(kc == KC - 1)),
                                    [ld1, bfree, h_evs[kc]] if kc == 0 else ([h_evs[kc]] if (j == 0 and fc == 0) else ()), sig=(kc == KC - 1))
                            ri, rt, rfree = rrot.get()
                            er = g.op("act", lambda e, rt=rt, ps=ps, j=j, fc=fc: e.activation(
                                rt[:], ps[:], AF.Relu, bias=b1s[:, l, j * 4 + fc:j * 4 + fc + 1], scale=1.0), [last1, rfree, ev_c])
                            g.bank_release(b, er)
                            eu = g.op("dve", lambda e, ut=ut, rt=rt, fc=fc: e.tensor_tensor(ut[:, fc, :], rt[:], rt[:], ALU.mult), [er, ufree])
                            rrot.release(ri, eu)
                            u_evs.append(eu)
                        w1rot.release(i1, last1)
                        for m in range(KC):
                            b, ps, bfree = g.bank()
                            for fc in range(4):
                                lastmm = g.op("pe", lambda e, ps=ps, w2t=w2t, fc=fc, m=m, ut=ut: e.matmul(
                                    ps[:], w2t[:, fc, m * 128:(m + 1) * 128], ut[:, fc, :], start=(fc == 0), stop=(fc == 3)),
                                    [ld2, bfree, u_evs] if fc == 0 else (), sig=(fc == 3))
                            ev = g.op("dve", lambda e, ps=ps, m=m: e.scalar_tensor_tensor(
                                Y[:, m, :], ps[:], modc(l, 5, m, slot), Y[:, m, :], ALU.mult, ALU.add), [lastmm, yb[m]])
                            g.bank_release(b, ev)
                            yb[m] = ev
                        w2rot.release(i2, lastmm)
                        urot.release(ui, lastmm)
                        issue_w12(t * 16 + j + 2)
                    h_free = lastmm
                    if not is_last:
                        for m in range(KC):
                            ych_free[m] = g.dma("sp", XTv[:, m, t0:t0 + 512], Y[:, m, :], "Y%d" % m, [yb[m]])
                    else:
                        rs_ev = norm_stats(Y, sqrot, yb, rstd, rstd_free)
                        fe = []
                        for kc in range(KC):
                            e1_ = g.op("dve", lambda e, kc=kc: e.tensor_tensor(Y[:, kc, :], Y[:, kc, :], rstd[:], ALU.mult), [rs_ev, yb[kc]])
                            fe.append(g.op("act", lambda e, kc=kc: e.activation(
                                Y[:, kc, :], Y[:, kc, :], AF.Copy, scale=fgs[:, kc:kc + 1]), [e1_, ev_c]))
                        rstd_free = fe[-1]
                        sts = []
                        lastt = None
                        for sub in range(4):
                            for hf in range(2):
                                oi, ot, ofree = orot.get()
                                oev = []
                                for q2 in range(2):
                                    q = hf * 2 + q2
                                    b, ps, bfree = g.bank()
                                    for jj in range(4):
                                        kc = q * 4 + jj
                                        lastt = g.op("pe", lambda e, ps=ps, kc=kc, jj=jj, sub=sub: e.transpose(
                                            ps[:, jj * 128:(jj + 1) * 128], Y[:, kc, sub * 128:(sub + 1) * 128], ident[:]),
                                            [fe, bfree, ev_c] if jj == 0 else (), sig=(jj == 3))
                                    if q % 2 == 0:
                                        ev = g.op("act", lambda e, ot=ot, ps=ps, q2=q2: e.copy(ot[:, q2 * 512:(q2 + 1) * 512], ps[:]), [lastt, ofree])
                                    else:
                                        ev = g.op("dve", lambda e, ot=ot, ps=ps, q2=q2: e.tensor_copy(ot[:, q2 * 512:(q2 + 1) * 512], ps[:]), [lastt, ofree])
                                    g.bank_release(b, ev)
                                    oev.append(ev)
                                st = g.dma("sp", y_out[t0 + sub * 128:t0 + (sub + 1) * 128, hf * 1024:(hf + 1) * 1024], ot[:], "S1_%d" % oi, oev)
                                orot.release(oi, st)
                                sts.append(st)
                        ych_free = [[lastt, sts]] * KC
            g.barrier()

        def phase_F1(l):
            with ExitStack() as _stk:
                f1x0 = _stk.enter_context(sbt("f1x0", [128, KC, 512], F32))
                f1x1 = _stk.enter_context(sbt("f1x1", [128, KC, 512], F32))
                H = _stk.enter_context(sbt("f1h", [128, KC, 512], BF16))
                cs = _stk.enter_context(sbt("f1cs", [128, 2, 512], BF16))
                f1m0 = _stk.enter_context(sbt("f1m0", [128, 4, 3, 128], BF16))
                f1m1 = _stk.enter_context(sbt("f1m1", [128, 4, 3, 128], BF16))
                f1g0 = _stk.enter_context(sbt("f1g0", [128, 8, 512], BF16))
                f1g1 = _stk.enter_context(sbt("f1g1", [128, 8, 512], BF16))
                f1z0 = _stk.enter_context(sbt("f1z0", [128, 8, 512], BF16))
                f1z1 = _stk.enter_context(sbt("f1z1", [128, 8, 512], BF16))
                sq0 = _stk.enter_context(sbt("f1sq0", [128, 512], BF16))
                sq1 = _stk.enter_context(sbt("f1sq1", [128, 512], BF16))
                tm0 = _stk.enter_context(sbt("f1tm0", [128, 512], F32))
                tm1 = _stk.enter_context(sbt("f1tm1", [128, 512], F32))
                rstd = _stk.enter_context(sbt("f1rs", [128, 512], F32))
                ecs = g.dma("sp", cs[:], cs_in, "L2_0")
                xrot = Rot([f1x0, f1x1]); mrot = Rot([f1m0, f1m1]); grot = Rot([f1g0, f1g1]); zrot = Rot([f1z0, f1z1])
                sqrot = Rot([sq0, sq1]); tmrot = Rot([tm0, tm1])
                rstd_free = None; h_free = None
                ZZv = ZZ.rearrange("k (tb q b) c -> tb q k b c", tb=16, q=4)
                for tb in range(16):
                    xi, xt, xfree = xrot.get()
                    ld = None
                    for a in range(4):
                        ld = g.dma("sp", xt[:, :, a * 128:(a + 1) * 128], XT4[:, :, a, tb * 128:(tb + 1) * 128], "L0_%d" % xi, [xfree] if a == 0 else ())
                    mi, mt_, mfree = mrot.get()
                    ldm = g.dma("sp", mt_[:], bfly_in[tb * 4:(tb + 1) * 4].rearrange("q m k n -> k q m n"), "L1_%d" % mi, [mfree])
                    rs_ev = norm_stats(xt, sqrot, ld, rstd, rstd_free)
                    h_evs = ada_apply(xt, H, rstd, rs_ev, tmrot, ld, h_free, GS1,
                                      lambda kc, s: modc(l, 0, kc, s), l, [(a * 128, (a + 1) * 128, a) for a in range(4)], remap=True)
                    xrot.release(xi, h_evs[-1])
                    rstd_free = h_evs[-1]
                    lastz = None
                    for q in range(4):
                        gi_, gt, gfree = grot.get()
                        g_evs = []
                        lastg = None
                        for gI in range(8):
                            b, ps, bfree = g.bank()
                            for k2 in range(2):
                                kc = gI * 2 + k2
                                lhs = H[:, kc, q * 128:(q + 1) * 128]
                                lastg = g.op("pe", lambda e, ps=ps, lhs=lhs, k2=k2: e.matmul(
                                    ps[:], lhs, cs[:, k2, :], start=(k2 == 0), stop=(k2 == 1)),
                                    [h_evs, ecs, bfree] if k2 == 0 else (), sig=(k2 == 1))
                            if gI % 2 == 0:
                                ev = g.op("act", lambda e, gt=gt, ps=ps, gI=gI: e.copy(gt[:, gI, :], ps[:]), [lastg, gfree])
                            else:
                                ev = g.op("dve", lambda e, gt=gt, ps=ps, gI=gI: e.tensor_copy(gt[:, gI, :], ps[:]), [lastg, gfree])
                            g.bank_release(b, ev)
                            g_evs.append(ev)
                        zi, zt, zfree = zrot.get()
                        z_evs = []
                        for gI in range(8):
                            b, ps, bfree = g.bank()
                            A_ = gt[:, gI, 0:256]
                            B_ = gt[:, gI, 256:512]
                            g.op("pe", lambda e, ps=ps, A_=A_, q=q: e.matmul(ps[:, 0:256], mt_[:, q, 0, :], A_, start=True, stop=False),
                                 [g_evs[gI], ldm, bfree], sig=False)
                            g.op("pe", lambda e, ps=ps, B_=B_, q=q: e.matmul(ps[:, 0:256], mt_[:, q, 1, :], B_, start=False, stop=True), (), sig=False)
                            g.op("pe", lambda e, ps=ps, A_=A_, q=q: e.matmul(ps[:, 256:512], mt_[:, q, 1, :], A_, start=True, stop=False), (), sig=False)
                            lastz = g.op("pe", lambda e, ps=ps, B_=B_, q=q: e.matmul(ps[:, 256:512], mt_[:, q, 2, :], B_, start=False, stop=True), (), sig=True)
                            if gI % 2 == 0:
                                ev = g.op("act", lambda e, zt=zt, ps=ps, gI=gI: e.copy(zt[:, gI, :], ps[:]), [lastz, zfree])
                            else:
                                ev = g.op("dve", lambda e, zt=zt, ps=ps, gI=gI: e.tensor_copy(zt[:, gI, :], ps[:]), [lastz, zfree])
                            g.bank_release(b, ev)
                            z_evs.append(ev)
                        grot.release(gi_, lastz)
                        st = None
                        for k1 in range(4):
                            st = g.dma("sp", ZZv[tb, q, k1], zt[32 * k1:32 * (k1 + 1)].rearrange("p g c -> p (g c)"), "S0_%d" % zi, z_evs if k1 == 0 else ())
                        zrot.release(zi, st)
                    mrot.release(mi, lastz)
                    h_free = lastz
            g.barrier()

        def phase_F2(l):
            with ExitStack() as _stk:
                Fr = _stk.enter_context(sbt("f2fr", [128, KC, 1024], BF16))
                Fi = _stk.enter_context(sbt("f2fi", [128, KC, 1024], BF16))
                f2z0 = _stk.enter_context(sbt("f2z0", [128, KC, 2, 512], BF16))
                f2z1 = _stk.enter_context(sbt("f2z1", [128, KC, 2, 512], BF16))
                f2y0 = _stk.enter_context(sbt("f2y0", [128, 8, 512], BF16))
                f2y1 = _stk.enter_context(sbt("f2y1", [128, 8, 512], BF16))
                zrot = Rot([f2z0, f2z1]); yrot = Rot([f2y0, f2y1])
                frv = fr_in.rearrange("(bc p) k -> p bc k", p=128)
                fiv = fi_in.rearrange("(bc p) k -> p bc k", p=128)
                f_free = None
                for kh in range(2):
                    ldf = [g.dma("sp", Fr[:], frv[:, :, kh * 1024:(kh + 1) * 1024], "L2_0", [f_free]),
                           g.dma("sp", Fi[:], fiv[:, :, kh * 1024:(kh + 1) * 1024], "L2_0")]
                    last = None
                    for k1 in range(NSLOT):
                        zsrc = ZZ[k1].rearrange("(bc p) c -> p bc c", p=128)
                        for ct in range(4):
                            zi, zt, zfree = zrot.get()
                            ldz = g.dma("sp", zt[:].rearrange("p b g c -> p b (g c)"), zsrc[:, :, ct * 1024:(ct + 1) * 1024], "L0_%d" % zi, [zfree])
                            yi, yt, yfree = yrot.get()
                            y_evs = []
                            for kk in range(8):
                                b, ps, bfree = g.bank()
                                for bc in range(KC):
                                    g.op("pe", lambda e, ps=ps, zt=zt, bc=bc, kk=kk: e.matmul(
                                        ps[:].rearrange("p (g c) -> p g c", g=2), Fr[:, bc, kk * 128:(kk + 1) * 128], zt[:, bc, :, 0:256],
                                        start=(bc == 0), stop=False), [ldf, ldz, bfree] if bc == 0 else (), sig=False)
                                    last = g.op("pe", lambda e, ps=ps, zt=zt, bc=bc, kk=kk: e.matmul(
                                        ps[:].rearrange("p (g c) -> p g c", g=2), Fi[:, bc, kk * 128:(kk + 1) * 128], zt[:, bc, :, 256:512],
                                        start=False, stop=(bc == KC - 1)), (), sig=(bc == KC - 1))
                                if kk % 2 == 0:
                                    ev = g.op("act", lambda e, yt=yt, ps=ps, kk=kk: e.copy(yt[:, kk, :], ps[:]), [last, yfree])
                                else:
                                    ev = g.op("dve", lambda e, yt=yt, ps=ps, kk=kk: e.tensor_copy(yt[:, kk, :], ps[:]), [last, yfree])
                                g.bank_release(b, ev)
                                y_evs.append(ev)
                            zrot.release(zi, last)
                            st = g.dma("sp", YY[k1, kh * 1024:(kh + 1) * 1024, ct * 512:(ct + 1) * 512].rearrange("(kk p) c -> p kk c", p=128),
                                       yt[:], "S0_%d" % yi, y_evs)
                            yrot.release(yi, st)
                    f_free = last
            g.barrier()

        for l in range(n_layers):
            is_last = final and (l == n_layers - 1)
            if l % 2 == 0:
                phase_A1(l)
                phase_A2(l)
                phase_tail(l, "attn", is_last)
            else:
                phase_F1(l)
                phase_F2(l)
                phase_tail(l, "fft", is_last)

        if dump_xt:
            with sbt("dmp", [128, KC, 512], F32) as dmp:
                fr = None
                for t in range(NTOK // 512):
                    ld = g.dma("sp", dmp[:], XTv[:, :, t * 512:(t + 1) * 512], "L7_0", [fr])
                    fr = g.dma("sp", xt_dump.rearrange("(kc p) t -> p kc t", p=128)[:, :, t * 512:(t + 1) * 512], dmp[:], "S2_0", [ld])
                g.barrier()
            with sbt("dmpy", [128, 16, D], BF16) as dmpy:
                fr = None
                for k1 in range(NSLOT):
                    ld = g.dma("sp", dmpy[:], YY[k1].rearrange("(r p) c -> p r c", p=128), "L7_0", [fr])
                    fr = g.dma("sp", yy_dump[k1 * SLOT:(k1 + 1) * SLOT, :].rearrange("(r p) c -> p r c", p=128), dmpy[:], "S2_0", [ld])
                g.barrier()
        g.barrier()
    return nc


def _bf(a):
    return np.ascontiguousarray(a.astype(ml_dtypes.bfloat16))


def _tables():
    tabs = {}
    tabs["ident"] = np.eye(128, dtype=np.float32)
    cin = (np.arange(2)[None, :, None] * 128 + np.arange(128)[:, None, None]).astype(np.float64)
    j = np.arange(256)[None, None, :].astype(np.float64)
    ang = 2 * np.pi * ((cin * j) % 256) / 256.0
    tabs["cstab"] = _bf(np.concatenate([np.cos(ang), np.sin(ang)], axis=2) / 16.0)
    b = np.arange(SLOT, dtype=np.int64)
    ang = 2 * np.pi * ((b[:, None] * b[None, :]) % SLOT).astype(np.float64) / SLOT
    tabs["frtab"] = _bf(np.cos(ang) / np.sqrt(SLOT))
    tabs["fitab"] = _bf(np.sin(ang) / np.sqrt(SLOT))
    slopes = np.exp2(-8.0 * np.arange(1, 19, dtype=np.float64) / 18.0)
    ab = np.zeros((3, 2, 128, 384), np.float32)
    kj = np.arange(192) - 64
    qi = np.arange(64)
    du = kj[:, None] - qi[None, :]
    valid = np.abs(du) <= 64
    for gi in range(3):
        for h in range(6):
            v = np.where(valid, -slopes[gi * 6 + h] * np.abs(du) * DIL[gi], NEG).astype(np.float32)
            ab[gi, 0, :, h * 64:(h + 1) * 64] = v[0:128]
            ab[gi, 1, 0:64, h * 64:(h + 1) * 64] = v[128:192]
    tabs["abias"] = ab
    sel = np.zeros((6, 6, 128), np.float32)
    for h in range(6):
        sel[h, h, :] = 1.0
    tabs["selh"] = sel
    return tabs


def _core_tables(is_prompt):
    pen = np.zeros((128, 9), np.float32)
    for s in range(4):
        first_masked = (not is_prompt) or s == 0
        last_masked = (not is_prompt) or s == 3
        if first_masked:
            pen[0:64, 1 + s] = NEG
        if last_masked:
            pen[0:64, 5 + s] = NEG
    bf = np.zeros((64, 3, 128, 128), np.float64)
    a = np.arange(4)
    for G_ in range(64):
        for bp in range(32):
            b = 32 * G_ + bp
            for k1 in range(4):
                if is_prompt:
                    psi = 2 * np.pi * ((b * k1) % 8192) / 8192.0 + 2 * np.pi * ((a * k1) % 4) / 4.0
                    tr = 0.5 * np.cos(psi)
                    ti = -0.5 * np.sin(psi)
                else:
                    tr = (a == k1).astype(np.float64)
                    ti = np.zeros(4)
                bf[G_, 0, a * 32 + bp, k1 * 32 + bp] = tr
                bf[G_, 1, a * 32 + bp, k1 * 32 + bp] = ti
                bf[G_, 2, a * 32 + bp, k1 * 32 + bp] = -tr
    fg = np.zeros((2, 128, 128), np.float32)
    if is_prompt:
        for k1 in range(4):
            for jp in range(32):
                fg[0, 32 * k1 + jp, 4 * jp + k1] = 1.0
    else:
        fg[1] = np.eye(128, dtype=np.float32)
    return pen, _bf(bf), _bf(fg)


def _colT(v, nchunk):
    v = np.asarray(v, np.float32)
    lead = v.shape[:-1]
    return np.ascontiguousarray(np.moveaxis(v.reshape(*lead, nchunk, 128), -1, 0))


def make_in_maps(inputs, cores=None):
    f32 = lambda a: np.ascontiguousarray(np.asarray(a, np.float32))
    xp = f32(inputs["x_prompt"]); xs = f32(inputs["x_sample"])
    cp = f32(inputs["c_prompt"]); cs = f32(inputs["c_sample"])
    shared = dict(_tables())
    shared["w_ada"] = f32(inputs["w_ada"]).reshape(NL * D, 6 * D)
    shared["b_adaT"] = _colT(inputs["b_ada"], 96)
    shared["n1T"] = _colT(inputs["norm1_g"], KC)
    shared["n2T"] = _colT(inputs["norm2_g"], KC)
    shared["w_qkv"] = f32(inputs["w_qkv"]).reshape(2 * D, QKVW)
    shared["w_o"] = f32(inputs["w_o"]).reshape(2 * AW, D)
    shared["w_f"] = f32(inputs["w_f"]).reshape(2 * D, D)
    shared["bfT"] = _colT(inputs["b_f"], KC)
    shared["w1"] = f32(inputs["w1"]).reshape(NL * D, DFF)
    shared["b1T"] = _colT(inputs["b1"], 64)
    shared["w2"] = f32(inputs["w2"]).reshape(NL * DFF, D)
    shared["b2T"] = _colT(inputs["b2"], KC)
    shared["fgT"] = _colT(inputs["final_g"], KC)
    tabs_p = _core_tables(True)
    tabs_s = _core_tables(False)
    maps = []
    for c in (range(N_CORES) if cores is None else cores):
        m = dict(shared)
        if c < 2:
            m["x"] = xp[c]
            cc = np.repeat(cp[c:c + 1], 4, axis=0)
            pen, bfly, fg = tabs_p
        else:
            cs_i = min(c, 5) - 2
            m["x"] = xs[4 * cs_i:4 * cs_i + 4].reshape(NTOK, D)
            cc = cs[4 * cs_i:4 * cs_i + 4]
            pen, bfly, fg = tabs_s
        m["cT"] = np.ascontiguousarray(np.transpose(cc.reshape(4, KC, 128), (2, 1, 0)))
        m["pen"] = pen
        m["bfly"] = bfly
        m["fg"] = fg
        maps.append(m)
    return maps


_NC_CACHE = {}


def kernel(**inputs):
    if "nc" not in _NC_CACHE:
        _NC_CACHE["nc"] = build_nc()
    nc = _NC_CACHE["nc"]
    in_maps = make_in_maps(inputs)
    res = run_bass_kernel_spmd(nc, in_maps, core_ids=list(range(N_CORES)))
    outs = [r["y"] for r in res.results]
    y_prompt = np.stack([outs[0], outs[1]], axis=0).astype(np.float32)
    y_sample = np.concatenate([outs[c].reshape(4, SLOT, D) for c in range(2, 6)], axis=0).astype(np.float32)
    return (y_prompt, y_sample)
```
